# Optimizing a Trainium2 kernel written in Bass

```python
import math
import jax
import jax.numpy as jnp
from jax import lax
import numpy as np

D_MODEL = 2048
BATCH = 4
SEQ = 4096
DEPTH = 2

GRID_W = 64
CTX_LEN = 256
BRANCH_W = D_MODEL // 2
N_BRANCH = 3
MLA_NOPE = 128
MLA_ROPE = 64
MLA_V = 128
MLA_HEADS = BRANCH_W // MLA_V
MLA_Q_LORA = 512
MLA_KV_LORA = 512
ROPE_THETA = 10000.0
Q_BLOCK = 128
RWKV_HEAD = 64
RWKV_HEADS = BRANCH_W // RWKV_HEAD
RWKV_W = BRANCH_W
RWKV_DECAY_LORA = 64
RWKV_A_LORA = 64
RWKV_GATE_LORA = 160
RWKV_CONV = 3
RWKV_GN_EPS = 64e-5
L2_EPS = 1e-12
S5_WIDTH = BRANCH_W
S5_GROUP = 16
S5_GROUPS = S5_WIDTH // S5_GROUP
S5_STATE = 64
S5_DT_MIN = 1e-3
S5_DT_MAX = 1e-1
D_FF = 4 * D_MODEL
NORM_EPS = 1e-6
IN_SPLITS = (MLA_Q_LORA, MLA_KV_LORA, MLA_ROPE,
             RWKV_W, RWKV_W, RWKV_W,
             RWKV_DECAY_LORA, RWKV_DECAY_LORA, RWKV_A_LORA, RWKV_A_LORA, RWKV_GATE_LORA,
             S5_WIDTH, N_BRANCH * D_MODEL)
N_IN = sum(IN_SPLITS)

kernel_name = 'hybrid_mla_rwkv7_s5_prefix_dit'


def rms_norm(x, g, eps=NORM_EPS):
    xf = x.astype(jnp.float32)
    y = xf * lax.rsqrt(jnp.mean(xf * xf, axis=-1, keepdims=True) + eps)
    return (y * g.astype(jnp.float32)).astype(x.dtype)


def modulate(h, shift, scale):
    return h * (1.0 + scale) + shift


def split_cols(z):
    out, start = [], 0
    for n in IN_SPLITS:
        out.append(z[..., start:start + n])
        start += n
    return out


def axial_rope_tables(rows, dtype):
    row = jnp.repeat(jnp.arange(rows, dtype=jnp.float32), GRID_W)
    col = jnp.tile(jnp.arange(GRID_W, dtype=jnp.float32), rows)
    n_freq = MLA_ROPE // 4
    inv = ROPE_THETA ** (-jnp.arange(n_freq, dtype=jnp.float32) / n_freq)
    ang = jnp.concatenate([row[:, None] * inv, col[:, None] * inv], axis=-1)
    return jnp.cos(ang).astype(dtype), jnp.sin(ang).astype(dtype)


def apply_rope(x, cos, sin):
    cos = cos[None, :, None, :]
    sin = sin[None, :, None, :]
    x1, x2 = jnp.split(x, 2, axis=-1)
    return jnp.concatenate([x1 * cos - x2 * sin, x1 * sin + x2 * cos], axis=-1)


def mla_queries(cq, q_lora_g, w_uq, qn_nope_g, qn_rope_g, rope):
    b, t, _ = cq.shape
    q = (rms_norm(cq, q_lora_g) @ w_uq).reshape(b, t, MLA_HEADS, MLA_NOPE + MLA_ROPE)
    q_nope = rms_norm(q[..., :MLA_NOPE], qn_nope_g)
    q_rope = rms_norm(q[..., MLA_NOPE:], qn_rope_g)
    if rope is not None:
        q_rope = apply_rope(q_rope, rope[0], rope[1])
    return jnp.concatenate([q_nope, q_rope], axis=-1)


def mla_keys_values(ckv, kr, kv_lora_g, w_ukv, kn_nope_g, kn_rope_g, rope):
    b, t, _ = ckv.shape
    kv = (rms_norm(ckv, kv_lora_g) @ w_ukv).reshape(b, t, MLA_HEADS, MLA_NOPE + MLA_V)
    k_nope = rms_norm(kv[..., :MLA_NOPE], kn_nope_g)
    k_rope = rms_norm(kr, kn_rope_g)[:, :, None, :]
    if rope is not None:
        k_rope = apply_rope(k_rope, rope[0], rope[1])
    k = jnp.concatenate([k_nope, jnp.broadcast_to(k_rope, (b, t, MLA_HEADS, MLA_ROPE))], axis=-1)
    return k, kv[..., MLA_NOPE:]


def softmax_attend(q, k, v):
    s = jnp.einsum('bqhd,bkhd->bhqk', q, k).astype(jnp.float32) * (1.0 / math.sqrt(q.shape[-1]))
    p = jax.nn.softmax(s, axis=-1).astype(v.dtype)
    return jnp.einsum('bhqk,bkhd->bqhd', p, v)


def blocked_attend(q, k, v):
    b, t, h, d = q.shape
    nb = t // Q_BLOCK
    qb = q.reshape(b, nb, Q_BLOCK, h, d).transpose(1, 0, 2, 3, 4)
    o = lax.map(lambda qi: softmax_attend(qi, k, v), qb)
    return o.transpose(1, 0, 2, 3, 4).reshape(b, t, h * v.shape[-1])


def centred_dwconv(x, w):
    return lax.conv_general_dilated(x, w[:, None, :].astype(x.dtype), window_strides=(1,), padding='SAME',
                                    dimension_numbers=('NWC', 'WIO', 'NWC'), feature_group_count=x.shape[-1])


def rwkv7_scan(r, decay, k, v, kk, a, s0, reverse, readout):
    f32 = jnp.float32
    xs = [jnp.swapaxes(z, 0, 1).astype(f32) for z in (r, decay, k, v, kk, a)]

    def step(S, inp):
        r_t, w_t, k_t, v_t, kk_t, a_t = inp
        sa = jnp.einsum('bhij,bhj->bhi', S, -kk_t)
        S = (S * w_t[:, :, None, :] + sa[..., None] * (kk_t * a_t)[:, :, None, :]
             + v_t[..., None] * k_t[:, :, None, :])
        y = jnp.einsum('bhij,bhj->bhi', S, r_t) if readout else None
        return S, y

    s, ys = lax.scan(step, s0, xs, reverse=reverse)
    return (jnp.swapaxes(ys, 0, 1) if readout else None), s


def rwkv_branch(r, k, v, wd, ad, gd, conv_w, w0, w2, a0, a2, g2, k_k, k_a, r_k, ln_g, ln_b, s0, readout):
    b, t, _ = r.shape
    heads = lambda z: z.reshape(b, t, RWKV_HEADS, RWKV_HEAD)
    r, k, v = jnp.split(centred_dwconv(jnp.concatenate([r, k, v], axis=-1), conv_w), 3, axis=-1)
    kk = heads(k * k_k).astype(jnp.float32)
    kk = kk * lax.rsqrt(jnp.sum(kk * kk, axis=-1, keepdims=True) + L2_EPS)
    ys, states, k_reps = [], [], []
    for d in range(2):
        w_log = -jax.nn.softplus(-(w0[d] + jnp.tanh(wd[d]) @ w2[d])) - 0.5
        decay = jnp.exp(-jnp.exp(w_log))
        a = jax.nn.sigmoid(a0[d] + ad[d] @ a2[d])
        k_rep = heads(k * (1.0 + (a - 1.0) * k_a))
        y, s = rwkv7_scan(heads(r), heads(decay), k_rep, heads(v), kk, heads(a), s0[d], d == 1, readout)
        ys.append(y)
        states.append(s)
        k_reps.append(k_rep)
    if not readout:
        return None, states
    y = ys[0] + ys[1]
    mu = jnp.mean(y, axis=-1, keepdims=True)
    var = jnp.mean(jnp.square(y - mu), axis=-1, keepdims=True)
    yn = ((y - mu) * lax.rsqrt(var + RWKV_GN_EPS)).reshape(b, t, RWKV_W) * ln_g + ln_b
    k_bonus = 0.5 * (k_reps[0] + k_reps[1])
    bonus = (jnp.sum(heads(r) * k_bonus * r_k, axis=-1, keepdims=True) * heads(v)).reshape(b, t, RWKV_W)
    out = (yn + bonus) * (jax.nn.sigmoid(gd) @ g2)
    return out.astype(r.dtype), states


def s5_discretise(lam_re, lam_im, log_dt, b_re, b_im):
    f32 = jnp.float32
    lam_re, lam_im, b_re, b_im = (z.astype(f32) for z in (lam_re, lam_im, b_re, b_im))
    dt = jnp.exp(log_dt.astype(f32))[:, None]
    mag = jnp.exp(lam_re * dt)
    a_re = mag * jnp.cos(lam_im * dt)
    a_im = mag * jnp.sin(lam_im * dt)
    den = lam_re * lam_re + lam_im * lam_im
    q_re = ((a_re - 1.0) * lam_re + a_im * lam_im) / den
    q_im = (a_im * lam_re - (a_re - 1.0) * lam_im) / den
    bb_re = q_re[..., None] * b_re - q_im[..., None] * b_im
    bb_im = q_re[..., None] * b_im + q_im[..., None] * b_re
    return a_re, a_im, bb_re, bb_im


def complex_linear_combine(e1, e2):
    a1r, a1i, b1r, b1i = e1
    a2r, a2i, b2r, b2i = e2
    return (a2r * a1r - a2i * a1i, a2r * a1i + a2i * a1r,
            a2r * b1r - a2i * b1i + b2r, a2r * b1i + a2i * b1r + b2i)


def s5_states(u, a_re, a_im, bb_re, bb_im, h0, reverse):
    bu_re = jnp.einsum('gpi,tbgi->tbgp', bb_re, u)
    bu_im = jnp.einsum('gpi,tbgi->tbgp', bb_im, u)
    if h0 is not None:
        h0_re, h0_im = h0
        first = -1 if reverse else 0
        bu_re = bu_re.at[first].add(a_re * h0_re - a_im * h0_im)
        bu_im = bu_im.at[first].add(a_re * h0_im + a_im * h0_re)
    t = u.shape[0]
    a_re_t = jnp.broadcast_to(a_re, (t, 1) + a_re.shape)
    a_im_t = jnp.broadcast_to(a_im, (t, 1) + a_im.shape)
    _, _, h_re, h_im = lax.associative_scan(complex_linear_combine, (a_re_t, a_im_t, bu_re, bu_im),
                                            reverse=reverse, axis=0)
    return h_re, h_im


def s5_branch(u, lam_re, lam_im, log_dt, b_re, b_im, c_re, c_im, d_skip, glu_w, glu_b, h0, readout):
    b, t, _ = u.shape
    f32 = jnp.float32
    ut = jnp.swapaxes(u, 0, 1).astype(f32).reshape(t, b, S5_GROUPS, S5_GROUP)
    ys, finals = [], []
    for d in range(2):
        rev = d == 1
        a_re, a_im, bb_re, bb_im = s5_discretise(lam_re[d], lam_im[d], log_dt[d], b_re[d], b_im[d])
        h_re, h_im = s5_states(ut, a_re, a_im, bb_re, bb_im, None if h0 is None else h0[d], rev)
        fin = 0 if rev else -1
        finals.append((h_re[fin], h_im[fin]))
        if readout:
            ys.append(jnp.einsum('gip,tbgp->tbgi', c_re[d].astype(f32), h_re)
                      - jnp.einsum('gip,tbgp->tbgi', c_im[d].astype(f32), h_im))
    if not readout:
        return None, finals
    y = jnp.swapaxes(ys[0] + ys[1], 0, 1).reshape(b, t, S5_WIDTH) + d_skip * u.astype(f32)
    z = jax.nn.gelu(y)
    out = z * jax.nn.sigmoid(z @ glu_w.astype(f32) + glu_b)
    return out.astype(u.dtype), finals


def gated_merge(gate_logits, branches, w_branch, w_out):
    b, t, _ = gate_logits.shape
    gates = jax.nn.sigmoid(gate_logits).reshape(b, t, N_BRANCH, D_MODEL)
    proj = jnp.einsum('btnc,ncd->btnd', jnp.stack(branches, axis=2), w_branch)
    return jnp.einsum('btnd,btnd->btd', gates, proj) @ w_out


def sq_relu_mlp(h, w1, w2):
    return jnp.square(jax.nn.relu(h @ w1)) @ w2


def setup_inputs(seed: int = 0) -> dict:
    key = jax.random.key(seed)
    ks = iter(jax.random.split(key, 48))
    f32 = jnp.float32
    L, D = DEPTH, D_MODEL

    def nrm(shape, scale):
        return scale * jax.random.normal(next(ks), shape, f32)

    def gain(shape):
        return 1.0 + nrm(shape, 0.02)

    n_frac = jnp.arange(RWKV_W, dtype=f32) / (RWKV_W - 1)
    depth_frac = jnp.arange(L, dtype=f32) / max(L - 1, 1)
    decay_speed = -7.0 + 5.0 * n_frac[None, :] ** (0.85 + depth_frac[:, None] ** 0.5)
    lam_im0 = math.pi * jnp.arange(S5_STATE, dtype=f32)
    s5_state_shape = (L, 2, S5_GROUPS, S5_STATE)
    return {
        'x': nrm((BATCH, SEQ, D), 1.0),
        'c': nrm((BATCH, D), 1.0),
        'ctx': nrm((BATCH, CTX_LEN, D), 1.0),
        'c_ctx': nrm((D,), 1.0),
        'ada_w': nrm((L, D, 6 * D), 0.5 * D ** -0.5),
        'ada_b': nrm((L, 6 * D), 0.02),
        'norm1_g': gain((L, D)),
        'norm2_g': gain((L, D)),
        'w_in': nrm((L, D, N_IN), D ** -0.5),
        'mla_q_lora_g': gain((L, MLA_Q_LORA)),
        'mla_kv_lora_g': gain((L, MLA_KV_LORA)),
        'mla_w_uq': nrm((L, MLA_Q_LORA, MLA_HEADS * (MLA_NOPE + MLA_ROPE)), MLA_Q_LORA ** -0.5),
        'mla_w_ukv': nrm((L, MLA_KV_LORA, MLA_HEADS * (MLA_NOPE + MLA_V)), MLA_KV_LORA ** -0.5),
        'mla_qn_nope_g': gain((L, MLA_NOPE)),
        'mla_qn_rope_g': gain((L, MLA_ROPE)),
        'mla_kn_nope_g': gain((L, MLA_NOPE)),
        'mla_kn_rope_g': gain((L, MLA_ROPE)),
        'rwkv_conv': 1.0 / RWKV_CONV + nrm((L, RWKV_CONV, 3 * RWKV_W), 0.1),
        'rwkv_w0': (decay_speed + 0.5)[:, None, :] + nrm((L, 2, RWKV_W), 0.05),
        'rwkv_w2': nrm((L, 2, RWKV_DECAY_LORA, RWKV_W), 0.1 * RWKV_DECAY_LORA ** -0.5),
        'rwkv_a0': nrm((L, 2, RWKV_W), 0.1),
        'rwkv_a2': nrm((L, 2, RWKV_A_LORA, RWKV_W), 0.5 * RWKV_A_LORA ** -0.5),
        'rwkv_g2': nrm((L, RWKV_GATE_LORA, RWKV_W), RWKV_GATE_LORA ** -0.5),
        'rwkv_k_k': 0.85 + nrm((L, RWKV_W), 0.02),
        'rwkv_k_a': 1.0 + nrm((L, RWKV_W), 0.02),
        'rwkv_r_k': nrm((L, RWKV_HEADS, RWKV_HEAD), 0.1),
        'rwkv_ln_g': gain((L, RWKV_W)),
        'rwkv_ln_b': nrm((L, RWKV_W), 0.02),
        's5_lam_re': -0.5 + nrm(s5_state_shape, 0.01),
        's5_lam_im': lam_im0 + nrm(s5_state_shape, 0.01),
        's5_log_dt': jax.random.uniform(next(ks), (L, 2, S5_GROUPS), f32,
                                        math.log(S5_DT_MIN), math.log(S5_DT_MAX)),
        's5_b_re': nrm((L, 2, S5_GROUPS, S5_STATE, S5_GROUP), (2 * S5_GROUP) ** -0.5),
        's5_b_im': nrm((L, 2, S5_GROUPS, S5_STATE, S5_GROUP), (2 * S5_GROUP) ** -0.5),
        's5_c_re': nrm((L, 2, S5_GROUPS, S5_GROUP, S5_STATE), (2 * S5_STATE) ** -0.5),
        's5_c_im': nrm((L, 2, S5_GROUPS, S5_GROUP, S5_STATE), (2 * S5_STATE) ** -0.5),
        's5_d': nrm((L, S5_WIDTH), 1.0),
        's5_glu_w': nrm((L, S5_WIDTH, S5_WIDTH), S5_WIDTH ** -0.5),
        's5_glu_b': nrm((L, S5_WIDTH), 0.02),
        'w_branch': nrm((L, N_BRANCH, BRANCH_W, D), BRANCH_W ** -0.5),
        'w_out': nrm((L, D, D), D ** -0.5),
        'w_mlp1': nrm((L, D, D_FF), D ** -0.5),
        'w_mlp2': nrm((L, D_FF, D), D_FF ** -0.5),
    }


def reference(x, c, ctx, c_ctx, ada_w, ada_b, norm1_g, norm2_g, w_in,
              mla_q_lora_g, mla_kv_lora_g, mla_w_uq, mla_w_ukv,
              mla_qn_nope_g, mla_qn_rope_g, mla_kn_nope_g, mla_kn_rope_g,
              rwkv_conv, rwkv_w0, rwkv_w2, rwkv_a0, rwkv_a2, rwkv_g2,
              rwkv_k_k, rwkv_k_a, rwkv_r_k, rwkv_ln_g, rwkv_ln_b,
              s5_lam_re, s5_lam_im, s5_log_dt, s5_b_re, s5_b_im, s5_c_re, s5_c_im,
              s5_d, s5_glu_w, s5_glu_b, w_branch, w_out, w_mlp1, w_mlp2):
    bsz, n_tok, _ = x.shape
    rows = n_tok // GRID_W
    rope = axial_rope_tables(rows, x.dtype)
    zero = jnp.zeros((bsz, RWKV_HEADS, RWKV_HEAD, RWKV_HEAD), jnp.float32)
    xc = ctx
    for l in range(DEPTH):
        last = l == DEPTH - 1
        mod_t = [m[:, None, :] for m in jnp.split(jax.nn.silu(c) @ ada_w[l] + ada_b[l], 6, axis=-1)]
        mod_c = jnp.split(jax.nn.silu(c_ctx) @ ada_w[l] + ada_b[l], 6, axis=-1)
        (cq_t, ckv_t, kr_t, r_t, k_t, v_t, wdf_t, wdb_t, adf_t, adb_t, gd_t, u_t, gate_t) = split_cols(
            modulate(rms_norm(x, norm1_g[l]), mod_t[0], mod_t[1]) @ w_in[l])
        (cq_c, ckv_c, kr_c, r_c, k_c, v_c, wdf_c, wdb_c, adf_c, adb_c, gd_c, u_c, gate_c) = split_cols(
            modulate(rms_norm(xc, norm1_g[l]), mod_c[0], mod_c[1]) @ w_in[l])

        key_c, val_c = mla_keys_values(ckv_c, kr_c, mla_kv_lora_g[l], mla_w_ukv[l],
                                       mla_kn_nope_g[l], mla_kn_rope_g[l], None)
        key_t, val_t = mla_keys_values(ckv_t, kr_t, mla_kv_lora_g[l], mla_w_ukv[l],
                                       mla_kn_nope_g[l], mla_kn_rope_g[l], rope)
        q_t = mla_queries(cq_t, mla_q_lora_g[l], mla_w_uq[l], mla_qn_nope_g[l], mla_qn_rope_g[l], rope)
        o_a_t = blocked_attend(q_t, jnp.concatenate([key_t, key_c], axis=1),
                               jnp.concatenate([val_t, val_c], axis=1))

        rwkv_p = (rwkv_conv[l], rwkv_w0[l], rwkv_w2[l], rwkv_a0[l], rwkv_a2[l], rwkv_g2[l],
                  rwkv_k_k[l], rwkv_k_a[l], rwkv_r_k[l], rwkv_ln_g[l], rwkv_ln_b[l])
        o_b_c, s_ctx = rwkv_branch(r_c, k_c, v_c, (wdf_c, wdb_c), (adf_c, adb_c), gd_c, *rwkv_p,
                                   (zero, zero), not last)
        o_b_t, _ = rwkv_branch(r_t, k_t, v_t, (wdf_t, wdb_t), (adf_t, adb_t), gd_t, *rwkv_p, s_ctx, True)

        s5_p = (s5_lam_re[l], s5_lam_im[l], s5_log_dt[l], s5_b_re[l], s5_b_im[l], s5_c_re[l], s5_c_im[l],
                s5_d[l], s5_glu_w[l], s5_glu_b[l])
        o_c_c, h_ctx = s5_branch(u_c, *s5_p, None, not last)
        o_c_t, _ = s5_branch(u_t, *s5_p, h_ctx, True)

        x = x + mod_t[2] * gated_merge(gate_t, (o_a_t, o_b_t, o_c_t), w_branch[l], w_out[l])
        x = x + mod_t[5] * sq_relu_mlp(modulate(rms_norm(x, norm2_g[l]), mod_t[3], mod_t[4]),
                                       w_mlp1[l], w_mlp2[l])

        if not last:
            q_c = mla_queries(cq_c, mla_q_lora_g[l], mla_w_uq[l], mla_qn_nope_g[l], mla_qn_rope_g[l], None)
            o_a_c = softmax_attend(q_c, key_c, val_c).reshape(bsz, xc.shape[1], BRANCH_W)
            xc = xc + mod_c[2] * gated_merge(gate_c, (o_a_c, o_b_c, o_c_c), w_branch[l], w_out[l])
            xc = xc + mod_c[5] * sq_relu_mlp(modulate(rms_norm(xc, norm2_g[l]), mod_c[3], mod_c[4]),
                                             w_mlp1[l], w_mlp2[l])
    return x
```

```python
import contextlib
import math
import numpy as np
import concourse.bass as bass
import concourse.mybir as mybir
from concourse.bass_utils import run_bass_kernel_spmd

F32 = mybir.dt.float32
F32R = mybir.dt.float32r
BF16 = mybir.dt.bfloat16
AF = mybir.ActivationFunctionType
ALU = mybir.AluOpType
AX = mybir.AxisListType

D = 2048
NB = 4
SEQ = 4096
CTX = 256
TT = SEQ + CTX
TC = TT // 2
NIN = 11744
DFF = 8192
EPS = 1e-6


class Tok:
    __slots__ = ("w", "r")

    def __init__(self):
        self.w = None
        self.r = {}


class KB:
    def __init__(self, auto_fence=True):
        self.auto_fence = auto_fence
        self.nc = bass.Bass("TRN2", target_bir_lowering=False)
        self.root = contextlib.ExitStack()
        self.es = self.root
        nc = self.nc
        self.eng = {"pe": nc.tensor, "dve": nc.vector, "act": nc.scalar, "pool": nc.gpsimd, "sp": nc.sync}
        self.sem = {}
        self.cnt = {}
        self.inc = {}
        self.seen = {k: {} for k in self.eng}
        self.nid = 0

    def _sem(self, key, inc):
        if key not in self.sem:
            self.sem[key] = self.root.enter_context(self.nc.semaphore("s_" + key))
            self.cnt[key] = 0
            self.inc[key] = inc
        return self.sem[key]

    def dram(self, name, shape, dt=F32, kind="ExternalInput"):
        return self.nc.dram_tensor(name, list(shape), dt, kind=kind).ap()

    def sb(self, shape, dt=F32, name=None):
        self.nid += 1
        return self.es.enter_context(self.nc.sbuf_tensor(name or f"sb{self.nid}", list(shape), dt))

    def ps(self, shape=(128, 512), dt=F32, name=None):
        self.nid += 1
        return self.es.enter_context(self.nc.psum_tensor(name or f"ps{self.nid}", list(shape), dt))

    def _issue(self, e, semkey, inc, fn, reads, writes):
        self._sem(semkey, inc)
        own = e
        need = {}
        for t in reads:
            if t.w is not None:
                k, c = t.w
                need[k] = max(need.get(k, 0), c)
        for t in writes:
            if t.w is not None:
                k, c = t.w
                need[k] = max(need.get(k, 0), c)
            for k, c in t.r.items():
                need[k] = max(need.get(k, 0), c)
        seen = self.seen[e]
        for k, c in need.items():
            if k == own:
                continue
            if seen.get(k, 0) >= c:
                continue
            self.eng[e].wait_ge(self.sem[k], c * self.inc[k])
            seen[k] = c
        ins = fn(self.eng[e])
        self.cnt[semkey] += 1
        c = self.cnt[semkey]
        ins.then_inc(self.sem[semkey], inc)
        for t in reads:
            t.r[semkey] = c
        for t in writes:
            t.w = (semkey, c)
            t.r = {}
        return ins

    def op(self, e, fn, reads=(), writes=(), fence=None):
        ins = self._issue(e, e, 1, fn, reads, writes)
        if fence is None:
            fence = self.auto_fence and e in ("dve", "act", "pool")
        if fence:
            if not hasattr(self, "_fscr"):
                self._fscr = self.sb([128, 8])
            scr = self._fscr
            col = {"dve": 0, "act": 1, "pool": 2, "pe": 3}[e]
            if e == "act":
                f2 = lambda g: g.activation(out=scr[:, col:col + 1], in_=scr[:, 4:5], func=AF.Copy)
            else:
                f2 = lambda g: g.memset(scr[:, col:col + 1], 0.0)
            self._issue(e, e, 1, f2, reads, writes)
        return ins

    def dma(self, e, out, in_, reads=(), writes=()):
        return self._issue(e, e + "_dma", 16, lambda g: g.dma_start(out=out, in_=in_), reads, writes)

    @contextlib.contextmanager
    def scope(self):
        outer = self.es
        self.es = contextlib.ExitStack()
        try:
            yield
        finally:
            self.es.close()
            self.es = outer

    def barrier(self, engines):
        for e in engines:
            for k, c in self.cnt.items():
                if c > 0 and k != e and self.seen[e].get(k, 0) < c:
                    self.eng[e].wait_ge(self.sem[k], c * self.inc[k])
                    self.seen[e][k] = c

    def finish(self):
        g = self.eng["sp"]
        for k, c in self.cnt.items():
            if c > 0:
                g.wait_ge(self.sem[k], c * self.inc[k])
        self.root.close()
        return self.nc


class Ring:
    def __init__(self, items):
        self.items = [(it, Tok()) for it in items]
        self.i = 0

    def next(self):
        it = self.items[self.i % len(self.items)]
        self.i += 1
        return it


def blocks(n, b=512):
    out = []
    s = 0
    while s < n:
        out.append((s, min(b, n - s)))
        s += b
    return out


def build_mods():
    kb = KB()
    NCOL = 3072
    cv = kb.dram("cv", [128, 16, 5])
    w = kb.dram("w", [D, NCOL])
    bias = kb.dram("bias", [5, NCOL])
    out = kb.dram("out", [5, NCOL], kind="ExternalOutput")
    cvt = kb.sb([128, 16, 5]); t_cv = Tok()
    st = kb.sb([128, 16, 5]); t_s = Tok()
    bt = kb.sb([5, NCOL]); t_b = Tok()
    ot = kb.sb([5, NCOL]); t_o = Tok()
    wring = Ring([kb.sb([128, 16, 512]) for _ in range(2)])
    pring = Ring([kb.ps() for _ in range(2)])
    kb.dma("sp", cvt[:], cv, writes=[t_cv])
    kb.dma("sp", bt[:], bias, writes=[t_b])
    kb.op("act", lambda g: g.activation(out=st[:], in_=cvt[:], func=AF.Silu), reads=[t_cv], writes=[t_s])
    wv = w.rearrange("(kc p) n -> p kc n", p=128)
    for bi, (c0, cn) in enumerate(blocks(NCOL)):
        wt, t_w = wring.next()
        kb.dma("sp", wt[:, :, :cn], wv[:, :, c0:c0 + cn], writes=[t_w])
        pt, t_p = pring.next()
        for kc in range(16):
            kb.op("pe", lambda g, kc=kc: g.matmul(pt[0:5, :cn], st[:, kc, :], wt[:, kc, :cn], start=(kc == 0), stop=(kc == 15)),
                  reads=[t_s, t_w], writes=[t_p])
        kb.op("dve", lambda g: g.tensor_tensor(out=ot[:, c0:c0 + cn], in0=pt[0:5, :cn], in1=bt[:, c0:c0 + cn], op=ALU.add),
              reads=[t_p, t_b], writes=[t_o])
    kb.dma("sp", out, ot[:], reads=[t_o])
    return kb.finish()


class Consts:
    def __init__(self, kb):
        self.kb = kb
        self.ones = kb.sb([128, 128]); self.t_ones = Tok()
        kb.op("pool", lambda g: g.memset(self.ones[:], 1.0), writes=[self.t_ones])
        self.ones_bf = kb.sb([128, 128], BF16); self.t_ones_bf = Tok()
        kb.op("pool", lambda g: g.memset(self.ones_bf[:], 1.0), writes=[self.t_ones_bf])


def emit_rstd(kb, cst, pring, sq_chunks, np_, n, inv_dim, eps, rstd_out, t_rstd, lhs=None, t_lhs=None):
    lhs = cst.ones[:np_, :np_] if lhs is None else lhs
    t_lhs = cst.t_ones if t_lhs is None else t_lhs
    pt, t_p = pring.next()
    for i, (ap, t) in enumerate(sq_chunks):
        kb.op("pe", lambda g, ap=ap, i=i: g.matmul(pt[:np_, :n], lhs, ap, start=(i == 0), stop=(i == len(sq_chunks) - 1)),
              reads=[t, t_lhs], writes=[t_p])
    kb.op("dve", lambda g: g.tensor_scalar(out=rstd_out, in0=pt[:np_, :n], scalar1=inv_dim, scalar2=eps, op0=ALU.mult, op1=ALU.add),
          reads=[t_p], writes=[t_rstd])
    kb.op("act", lambda g: g.activation(out=rstd_out, in_=rstd_out, func=AF.Sqrt), reads=[t_rstd], writes=[t_rstd])
    kb.op("dve", lambda g: g.reciprocal(out=rstd_out, in_=rstd_out), reads=[t_rstd], writes=[t_rstd])


def emit_norm_mod(kb, cst, xT, gs, t_gs, sh, t_sh, xn, t_xn, pring, nseg_a):
    n = TC
    xv = xT.rearrange("(kc p) t -> p kc t", p=128)
    xring = Ring([kb.sb([128, n]) for _ in range(3)])
    sqring = Ring([kb.sb([128, n]) for _ in range(2)])
    blks = blocks(n)
    pts = [pring.next() for _ in blks]
    for kc in range(16):
        xt, t_x = xring.next()
        kb.dma("sp", xt[:], xv[:, kc, :], writes=[t_x])
        sq, t_q = sqring.next()
        kb.op("act", lambda g: g.activation(out=sq[:], in_=xt[:], func=AF.Square), reads=[t_x], writes=[t_q])
        for (c0, cn), (pt, t_p) in zip(blks, pts):
            kb.op("pe", lambda g, pt=pt, c0=c0, cn=cn: g.matmul(pt[:, :cn], cst.ones[:], sq[:, c0:c0 + cn], start=(kc == 0), stop=(kc == 15)),
                  reads=[t_q, cst.t_ones], writes=[t_p])
    rstd = kb.sb([128, n]); t_r = Tok()
    for (c0, cn), (pt, t_p) in zip(blks, pts):
        kb.op("dve", lambda g, pt=pt, c0=c0, cn=cn: g.tensor_scalar(out=rstd[:, c0:c0 + cn], in0=pt[:, :cn], scalar1=1.0 / D, scalar2=EPS, op0=ALU.mult, op1=ALU.add),
              reads=[t_p], writes=[t_r])
    kb.op("act", lambda g: g.activation(out=rstd[:], in_=rstd[:], func=AF.Sqrt), reads=[t_r], writes=[t_r])
    kb.op("dve", lambda g: g.reciprocal(out=rstd[:], in_=rstd[:]), reads=[t_r], writes=[t_r])
    for kc in range(16):
        xt, t_x = xring.next()
        kb.dma("sp", xt[:], xv[:, kc, :], writes=[t_x])
        kb.op("dve", lambda g: g.tensor_tensor(out=xt[:], in0=xt[:], in1=rstd[:], op=ALU.mult), reads=[t_x, t_r], writes=[t_x])
        for seg, (c0, c1) in enumerate(((0, nseg_a), (nseg_a, n))):
            kb.op("act", lambda g, seg=seg, c0=c0, c1=c1: g.activation(out=xn[:, kc, c0:c1], in_=xt[:, c0:c1], func=AF.Identity,
                                                                    scale=gs[:, kc, seg:seg + 1], bias=sh[:, kc, seg:seg + 1]),
                  reads=[t_x, t_gs, t_sh], writes=[t_xn])


def emit_linear_stream(kb, xn, t_xn, nk, w, col0, ncols, n_tok, pring, wring_f, wring_b, oring, out_dram, evac=None):
    wv = w.rearrange("(kc p) n -> p kc n", p=128)
    blks = blocks(n_tok)
    PW = 256
    ei = 0
    for p0 in range(0, ncols, PW):
        pw = min(PW, ncols - p0)
        wf, t_wf = wring_f.next()
        kb.dma("sp", wf[:, :, :pw], wv[:, :, col0 + p0:col0 + p0 + pw], writes=[t_wf])
        wb, t_wb = wring_b.next()
        kb.op("pool", lambda g: g.tensor_copy(out=wb[:, :, :pw], in_=wf[:, :, :pw]), reads=[t_wf], writes=[t_wb])
        for m0 in range(0, pw, 128):
            m = min(128, pw - m0)
            ot, t_o = oring.next()
            for (c0, cn) in blks:
                pt, t_p = pring.next()
                for kc in range(nk):
                    kb.op("pe", lambda g, kc=kc, pt=pt, c0=c0, cn=cn: g.matmul(pt[:m, :cn], wb[:, kc, m0:m0 + m], xn[:, kc, c0:c0 + cn],
                                                                             start=(kc == 0), stop=(kc == nk - 1)),
                          reads=[t_wb, t_xn], writes=[t_p])
                e = "act" if ei % 2 == 0 else "dve"
                ei += 1
                if e == "act":
                    kb.op("act", lambda g, pt=pt, c0=c0, cn=cn: g.activation(out=ot[:m, c0:c0 + cn], in_=pt[:m, :cn], func=AF.Copy), reads=[t_p], writes=[t_o])
                else:
                    kb.op("dve", lambda g, pt=pt, c0=c0, cn=cn: g.tensor_copy(out=ot[:m, c0:c0 + cn], in_=pt[:m, :cn]), reads=[t_p], writes=[t_o])
            kb.dma("pool", out_dram[p0 + m0:p0 + m0 + m, :], ot[:m, :], reads=[t_o])


def build_la():
    kb = KB()
    xT = kb.dram("xT", [D, TC])
    modv = kb.dram("modv", [128, 16, 2, 2])
    g1 = kb.dram("g1", [128, 16])
    w_in = kb.dram("w_in", [D, NIN])
    zT = kb.dram("zT", [NIN, TC], kind="ExternalOutput")
    cst = Consts(kb)
    pring = Ring([kb.ps() for _ in range(8)])
    mv = kb.sb([128, 16, 2, 2]); t_mv = Tok()
    g1t = kb.sb([128, 16]); t_g1 = Tok()
    kb.dma("sp", mv[:], modv, writes=[t_mv])
    kb.dma("sp", g1t[:], g1, writes=[t_g1])
    gs = kb.sb([128, 16, 2]); t_gs = Tok()
    sh = kb.sb([128, 16, 2]); t_sh = Tok()
    for seg in range(2):
        kb.op("dve", lambda g, seg=seg: g.scalar_tensor_tensor(out=gs[:, :, seg], in0=mv[:, :, seg, 1], scalar=1.0, in1=g1t[:], op0=ALU.add, op1=ALU.mult),
              reads=[t_mv, t_g1], writes=[t_gs])
        kb.op("dve", lambda g, seg=seg: g.tensor_copy(out=sh[:, :, seg], in_=mv[:, :, seg, 0]), reads=[t_mv], writes=[t_sh])
    xn = kb.sb([128, 16, TC], BF16); t_xn = Tok()
    emit_norm_mod(kb, cst, xT, gs, t_gs, sh, t_sh, xn, t_xn, pring, CTX)
    wring_f = Ring([kb.sb([128, 16, 256]) for _ in range(2)])
    wring_b = Ring([kb.sb([128, 16, 256], BF16) for _ in range(2)])
    oring = Ring([kb.sb([128, TC]) for _ in range(2)])
    emit_linear_stream(kb, xn, t_xn, 16, w_in, 0, NIN, TC, pring, wring_f, wring_b, oring, zT)
    return kb.finish()


_cache = {}


def get_prog(name, fn):
    if name not in _cache:
        _cache[name] = fn()
    return _cache[name]


def fm(v):
    return np.ascontiguousarray(v.reshape(-1, 128).T)


def run(nc, in_maps):
    res = run_bass_kernel_spmd(nc, in_maps, core_ids=list(range(8)))
    return res.results


class LinRes:
    def __init__(self, kb, n_ps=6):
        self.wf = Ring([kb.sb([128, 16, 256]) for _ in range(2)])
        self.wb = Ring([kb.sb([128, 16, 256], BF16) for _ in range(2)])
        self.ps = Ring([kb.ps() for _ in range(n_ps)])
        self.qi = 0


def emit_linear(kb, lr, src_chunks, w, M, n, evac, col0=0):
    PW = 256
    nk = len(src_chunks)
    groups = [list(range(s, min(s + 16, nk))) for s in range(0, nk, 16)]
    for p0 in range(0, M, PW):
        pw = min(PW, M - p0)
        mcs = [(m0, min(128, pw - m0)) for m0 in range(0, pw, 128)]
        pts = [lr.ps.next() for _ in mcs]
        for gi, grp in enumerate(groups):
            wf, t_wf = lr.wf.next()
            nfull = sum(1 for k in grp if src_chunks[k][2] == 128)
            r0 = grp[0] * 128
            q = "sp" if lr.qi % 2 == 0 else "act"
            lr.qi += 1
            if nfull:
                kb.dma(q, wf[:, :nfull, :pw], w[r0:r0 + nfull * 128, col0 + p0:col0 + p0 + pw].rearrange("(kc p) n -> p kc n", p=128), writes=[t_wf])
            for k in grp[nfull:]:
                kp = src_chunks[k][2]
                kb.dma(q, wf[:kp, k - grp[0], :pw], w[k * 128:k * 128 + kp, col0 + p0:col0 + p0 + pw], writes=[t_wf])
            wb, t_wb = lr.wb.next()
            if nfull:
                kb.op("pool", lambda g: g.tensor_copy(out=wb[:, :nfull, :pw], in_=wf[:, :nfull, :pw]), reads=[t_wf], writes=[t_wb])
            for k in grp[nfull:]:
                kp = src_chunks[k][2]
                kb.op("pool", lambda g, k=k, kp=kp: g.tensor_copy(out=wb[:kp, k - grp[0], :pw], in_=wf[:kp, k - grp[0], :pw]), reads=[t_wf], writes=[t_wb])
            for (m0, m), (pt, t_p) in zip(mcs, pts):
                for k in grp:
                    ap, t_s, kp = src_chunks[k]
                    kb.op("pe", lambda g, k=k, ap=ap, kp=kp, pt=pt, m0=m0, m=m: g.matmul(pt[:m, :n], wb[:kp, k - grp[0], m0:m0 + m], ap,
                                                                                      start=(k == 0), stop=(k == nk - 1)),
                          reads=[t_wb, t_s], writes=[t_p])
        for (m0, m), (pt, t_p) in zip(mcs, pts):
            evac((p0 + m0) // 128, m, pt, t_p)


LC_BLOCKS = [(0, 256, 0)] + [(c, 256, 1) for c in range(256, 2048, 256)] + [(1920, 256, 1)]
GELU_C = 0.7978845608028654


def build_lc():
    kb = KB()
    xT = kb.dram("xT", [D, TC])
    oaT = kb.dram("oaT", [1024, TC])
    rw = kb.dram("rw", [7, 1024, TC])
    gdT = kb.dram("gdT", [160, TC])
    s5 = kb.dram("s5", [3, 1024, TC])
    gateT = kb.dram("gateT", [6144, TC])
    modv = kb.dram("modv", [128, 16, 2, 4])
    g2n = kb.dram("g2n", [128, 16])
    rvec = kb.dram("rvec", [128, 8, 6])
    bdm = kb.dram("bdm", [128, 128])
    w_g2 = kb.dram("w_g2", [160, 1024])
    w_glu = kb.dram("w_glu", [1024, 1024])
    w_br = kb.dram("w_br", [3072, D])
    w_out = kb.dram("w_out", [D, D])
    w1 = kb.dram("w1", [D, DFF])
    w2 = kb.dram("w2", [DFF, D])
    outT = kb.dram("outT", [D, TC], kind="ExternalOutput")
    cst = Consts(kb)
    lr = LinRes(kb, 6)
    sring = Ring([kb.ps() for _ in range(2)])
    N = 256
    mv = kb.sb([128, 16, 2, 4]); t_mv = Tok()
    g2t = kb.sb([128, 16]); t_g2 = Tok()
    rv = kb.sb([128, 8, 6]); t_rv = Tok()
    bd = kb.sb([128, 128]); t_bd = Tok()
    kb.dma("sp", mv[:], modv, writes=[t_mv])
    kb.dma("sp", g2t[:], g2n, writes=[t_g2])
    kb.dma("sp", rv[:], rvec, writes=[t_rv])
    kb.dma("sp", bd[:], bdm, writes=[t_bd])
    gs = kb.sb([128, 16, 2]); t_gs = Tok()
    for seg in range(2):
        kb.op("dve", lambda g, seg=seg: g.scalar_tensor_tensor(out=gs[:, :, seg], in0=mv[:, :, seg, 2], scalar=1.0, in1=g2t[:], op0=ALU.add, op1=ALU.mult),
              reads=[t_mv, t_g2], writes=[t_gs])
    x = kb.sb([128, 16, N]); t_x = [Tok() for _ in range(16)]
    ld = Ring([kb.sb([128, N]) for _ in range(8)])
    tmp = Ring([kb.sb([128, N]) for _ in range(6)])
    oa = kb.sb([128, 8, N], BF16); t_oa = [Tok() for _ in range(8)]
    ob = kb.sb([128, 8, N], BF16); t_ob = [Tok() for _ in range(8)]
    oc = kb.sb([128, 8, N], BF16); t_oc = [Tok() for _ in range(8)]
    z32 = kb.sb([128, 8, N]); zb = kb.sb([128, 8, N], BF16); t_z = [Tok() for _ in range(8)]
    sg = kb.sb([128, 2, N], BF16); t_sg = Tok()
    acc = kb.sb([128, 16, N]); t_acc = [Tok() for _ in range(16)]
    mg = kb.sb([128, 16, N], BF16); t_mg = [Tok() for _ in range(16)]
    hid = kb.sb([128, 64, N], BF16); t_hid = [Tok() for _ in range(64)]
    rstd = kb.sb([128, N]); t_rstd = Tok()

    def load(src_ap, n, rows=128, q="sp"):
        t, tk = ld.next()
        kb.dma(q, t[:rows, :n], src_ap, writes=[tk])
        return t, tk

    for (c0, n, seg) in LC_BLOCKS:
        cs = slice(c0, c0 + n)
        for kc in range(16):
            kb.dma("sp", x[:, kc, :n], xT[kc * 128:(kc + 1) * 128, cs], writes=[t_x[kc]])
        for pc in range(8):
            t, tk = load(oaT[pc * 128:(pc + 1) * 128, cs], n)
            kb.op("act", lambda g: g.activation(out=oa[:, pc, :n], in_=t[:, :n], func=AF.Copy), reads=[tk], writes=[t_oa[pc]])
        t, tk = load(gdT[0:128, cs], n)
        kb.op("act", lambda g: g.activation(out=sg[:, 0, :n], in_=t[:, :n], func=AF.Sigmoid), reads=[tk], writes=[t_sg])
        t, tk = load(gdT[128:160, cs], n, rows=32)
        kb.op("act", lambda g: g.activation(out=sg[:32, 1, :n], in_=t[:32, :n], func=AF.Sigmoid), reads=[tk], writes=[t_sg])
        obf = {}
        for pc in range(8):
            rs = slice(pc * 128, (pc + 1) * 128)
            yf, k_yf = load(rw[0, rs, cs], n)
            yb, k_yb = load(rw[1, rs, cs], n, q="act")
            y, k_y = tmp.next()
            kb.op("dve", lambda g: g.tensor_tensor(out=y[:, :n], in0=yf[:, :n], in1=yb[:, :n], op=ALU.add), reads=[k_yf, k_yb], writes=[k_y])
            pm, k_pm = sring.next()
            kb.op("pe", lambda g: g.matmul(pm[:, :n], bd[:], y[:, :n], start=True, stop=True), reads=[t_bd, k_y], writes=[k_pm])
            kb.op("dve", lambda g: g.scalar_tensor_tensor(out=y[:, :n], in0=pm[:, :n], scalar=-1.0 / 64, in1=y[:, :n], op0=ALU.mult, op1=ALU.add),
                  reads=[k_pm, k_y], writes=[k_y])
            sq, k_sq = tmp.next()
            kb.op("act", lambda g: g.activation(out=sq[:, :n], in_=y[:, :n], func=AF.Square), reads=[k_y], writes=[k_sq])
            pv, k_pv = sring.next()
            kb.op("pe", lambda g: g.matmul(pv[:, :n], bd[:], sq[:, :n], start=True, stop=True), reads=[t_bd, k_sq], writes=[k_pv])
            kb.op("dve", lambda g: g.tensor_scalar(out=sq[:, :n], in0=pv[:, :n], scalar1=1.0 / 64, scalar2=64e-5, op0=ALU.mult, op1=ALU.add),
                  reads=[k_pv], writes=[k_sq])
            kb.op("act", lambda g: g.activation(out=sq[:, :n], in_=sq[:, :n], func=AF.Sqrt), reads=[k_sq], writes=[k_sq])
            kb.op("dve", lambda g: g.reciprocal(out=sq[:, :n], in_=sq[:, :n]), reads=[k_sq], writes=[k_sq])
            kb.op("dve", lambda g: g.tensor_tensor(out=y[:, :n], in0=y[:, :n], in1=sq[:, :n], op=ALU.mult), reads=[k_y, k_sq], writes=[k_y])
            kb.op("act", lambda g: g.activation(out=y[:, :n], in_=y[:, :n], func=AF.Identity, scale=rv[:, pc, 0:1], bias=rv[:, pc, 1:2]),
                  reads=[k_y, t_rv], writes=[k_y])
            af, k_af = load(rw[5, rs, cs], n)
            ab, k_ab = load(rw[6, rs, cs], n, q="act")
            kc_, k_kc = load(rw[3, rs, cs], n)
            rc_, k_rc = load(rw[2, rs, cs], n, q="act")
            vc_, k_vc = load(rw[4, rs, cs], n)
            b1, k_b1 = tmp.next()
            kb.op("pool", lambda g: g.tensor_tensor(out=b1[:, :n], in0=af[:, :n], in1=ab[:, :n], op=ALU.add), reads=[k_af, k_ab], writes=[k_b1])
            kb.op("dve", lambda g: g.tensor_scalar(out=b1[:, :n], in0=b1[:, :n], scalar1=0.5, scalar2=-1.0, op0=ALU.mult, op1=ALU.add), reads=[k_b1], writes=[k_b1])
            kb.op("dve", lambda g: g.tensor_scalar(out=b1[:, :n], in0=b1[:, :n], scalar1=rv[:, pc, 2:3], scalar2=1.0, op0=ALU.mult, op1=ALU.add),
                  reads=[k_b1, t_rv], writes=[k_b1])
            kb.op("dve", lambda g: g.tensor_tensor(out=b1[:, :n], in0=b1[:, :n], in1=kc_[:, :n], op=ALU.mult), reads=[k_b1, k_kc], writes=[k_b1])
            kb.op("dve", lambda g: g.scalar_tensor_tensor(out=b1[:, :n], in0=b1[:, :n], scalar=rv[:, pc, 3:4], in1=rc_[:, :n], op0=ALU.mult, op1=ALU.mult),
                  reads=[k_b1, k_rc, t_rv], writes=[k_b1])
            pb, k_pb = sring.next()
            kb.op("pe", lambda g: g.matmul(pb[:, :n], bd[:], b1[:, :n], start=True, stop=True), reads=[t_bd, k_b1], writes=[k_pb])
            kb.op("dve", lambda g: g.tensor_tensor(out=b1[:, :n], in0=pb[:, :n], in1=vc_[:, :n], op=ALU.mult), reads=[k_pb, k_vc], writes=[k_b1])
            kb.op("dve", lambda g: g.tensor_tensor(out=y[:, :n], in0=y[:, :n], in1=b1[:, :n], op=ALU.add), reads=[k_y, k_b1], writes=[k_y])
            obf[pc] = (y, k_y)

            def ev(mc, m, pt, t_p, y=y, k_y=k_y, pc=pc):
                kb.op("dve", lambda g: g.tensor_tensor(out=ob[:, pc, :n], in0=pt[:, :n], in1=y[:, :n], op=ALU.mult), reads=[t_p, k_y], writes=[t_ob[pc]])
            emit_linear(kb, lr, [(sg[:, 0, :n], t_sg, 128), (sg[:32, 1, :n], t_sg, 32)], w_g2, 128, n, ev, col0=pc * 128)
        for pc in range(8):
            rs = slice(pc * 128, (pc + 1) * 128)
            yf, k_yf = load(s5[0, rs, cs], n)
            yb, k_yb = load(s5[1, rs, cs], n, q="act")
            u_, k_u = load(s5[2, rs, cs], n)
            y, k_y = tmp.next()
            kb.op("pool", lambda g: g.tensor_tensor(out=y[:, :n], in0=yf[:, :n], in1=yb[:, :n], op=ALU.add), reads=[k_yf, k_yb], writes=[k_y])
            kb.op("dve", lambda g: g.scalar_tensor_tensor(out=y[:, :n], in0=u_[:, :n], scalar=rv[:, pc, 4:5], in1=y[:, :n], op0=ALU.mult, op1=ALU.add),
                  reads=[k_u, k_y, t_rv], writes=[k_y])
            t2, k_t2 = tmp.next()
            kb.op("act", lambda g: g.activation(out=t2[:, :n], in_=y[:, :n], func=AF.Square), reads=[k_y], writes=[k_t2])
            kb.op("dve", lambda g: g.tensor_scalar(out=t2[:, :n], in0=t2[:, :n], scalar1=0.044715, scalar2=1.0, op0=ALU.mult, op1=ALU.add), reads=[k_t2], writes=[k_t2])
            kb.op("dve", lambda g: g.tensor_tensor(out=t2[:, :n], in0=t2[:, :n], in1=y[:, :n], op=ALU.mult), reads=[k_t2, k_y], writes=[k_t2])
            kb.op("act", lambda g: g.activation(out=t2[:, :n], in_=t2[:, :n], func=AF.Tanh, scale=GELU_C), reads=[k_t2], writes=[k_t2])
            kb.op("dve", lambda g: g.tensor_scalar(out=t2[:, :n], in0=t2[:, :n], scalar1=0.5, scalar2=0.5, op0=ALU.mult, op1=ALU.add), reads=[k_t2], writes=[k_t2])
            kb.op("dve", lambda g: g.tensor_tensor(out=z32[:, pc, :n], in0=t2[:, :n], in1=y[:, :n], op=ALU.mult), reads=[k_t2, k_y], writes=[t_z[pc]])
            kb.op("act", lambda g: g.activation(out=zb[:, pc, :n], in_=z32[:, pc, :n], func=AF.Copy), reads=[t_z[pc]], writes=[t_z[pc]])

        def ev_glu(mc, m, pt, t_p):
            s_, k_s = tmp.next()
            kb.op("act", lambda g: g.activation(out=s_[:, :n], in_=pt[:, :n], func=AF.Sigmoid, bias=rv[:, mc, 5:6], scale=1.0), reads=[t_p, t_rv], writes=[k_s])
            kb.op("dve", lambda g: g.tensor_tensor(out=oc[:, mc, :n], in0=s_[:, :n], in1=z32[:, mc, :n], op=ALU.mult), reads=[k_s, t_z[mc]], writes=[t_oc[mc]])
        emit_linear(kb, lr, [(zb[:, k, :n], t_z[k], 128) for k in range(8)], w_glu, 1024, n, ev_glu)
        for nb_, (br, t_br) in enumerate(((oa, t_oa), (ob, t_ob), (oc, t_oc))):
            def ev_br(mc, m, pt, t_p, nb_=nb_):
                gt, k_g = load(gateT[nb_ * D + mc * 128: nb_ * D + (mc + 1) * 128, cs], n, q="act")
                kb.op("act", lambda g: g.activation(out=gt[:, :n], in_=gt[:, :n], func=AF.Sigmoid), reads=[k_g], writes=[k_g])
                if nb_ == 0:
                    kb.op("dve", lambda g: g.tensor_tensor(out=acc[:, mc, :n], in0=pt[:, :n], in1=gt[:, :n], op=ALU.mult), reads=[t_p, k_g], writes=[t_acc[mc]])
                else:
                    kb.op("dve", lambda g: g.tensor_tensor(out=gt[:, :n], in0=pt[:, :n], in1=gt[:, :n], op=ALU.mult), reads=[t_p, k_g], writes=[k_g])
                    kb.op("pool", lambda g: g.tensor_tensor(out=acc[:, mc, :n], in0=acc[:, mc, :n], in1=gt[:, :n], op=ALU.add), reads=[k_g, t_acc[mc]], writes=[t_acc[mc]])
                if nb_ == 2:
                    kb.op("act", lambda g: g.activation(out=mg[:, mc, :n], in_=acc[:, mc, :n], func=AF.Copy), reads=[t_acc[mc]], writes=[t_mg[mc]])
            emit_linear(kb, lr, [(br[:, k, :n], t_br[k], 128) for k in range(8)], w_br[nb_ * 1024:(nb_ + 1) * 1024, :], D, n, ev_br)

        def ev_out(mc, m, pt, t_p):
            kb.op("dve", lambda g: g.scalar_tensor_tensor(out=x[:, mc, :n], in0=pt[:, :n], scalar=mv[:, mc, seg, 0:1], in1=x[:, mc, :n], op0=ALU.mult, op1=ALU.add),
                  reads=[t_p, t_mv, t_x[mc]], writes=[t_x[mc]])
        emit_linear(kb, lr, [(mg[:, k, :n], t_mg[k], 128) for k in range(16)], w_out, D, n, ev_out)
        pt, t_p = sring.next()
        for kc in range(16):
            sq, k_sq = tmp.next()
            kb.op("act", lambda g: g.activation(out=sq[:, :n], in_=x[:, kc, :n], func=AF.Square), reads=[t_x[kc]], writes=[k_sq])
            kb.op("pe", lambda g: g.matmul(pt[:, :n], cst.ones[:], sq[:, :n], start=(kc == 0), stop=(kc == 15)), reads=[k_sq, cst.t_ones], writes=[t_p])
        kb.op("dve", lambda g: g.tensor_scalar(out=rstd[:, :n], in0=pt[:, :n], scalar1=1.0 / D, scalar2=EPS, op0=ALU.mult, op1=ALU.add), reads=[t_p], writes=[t_rstd])
        kb.op("act", lambda g: g.activation(out=rstd[:, :n], in_=rstd[:, :n], func=AF.Sqrt), reads=[t_rstd], writes=[t_rstd])
        kb.op("dve", lambda g: g.reciprocal(out=rstd[:, :n], in_=rstd[:, :n]), reads=[t_rstd], writes=[t_rstd])
        for kc in range(16):
            t_, k_t = tmp.next()
            kb.op("dve", lambda g: g.tensor_tensor(out=t_[:, :n], in0=x[:, kc, :n], in1=rstd[:, :n], op=ALU.mult), reads=[t_x[kc], t_rstd], writes=[k_t])
            kb.op("act", lambda g: g.activation(out=mg[:, kc, :n], in_=t_[:, :n], func=AF.Identity, scale=gs[:, kc, seg:seg + 1], bias=mv[:, kc, seg, 1:2]),
                  reads=[k_t, t_gs, t_mv], writes=[t_mg[kc]])

        def ev_m1(mc, m, pt, t_p):
            t_, k_t = tmp.next()
            kb.op("act", lambda g: g.activation(out=t_[:, :n], in_=pt[:, :n], func=AF.Relu), reads=[t_p], writes=[k_t])
            kb.op("pool", lambda g: g.tensor_tensor(out=hid[:, mc, :n], in0=t_[:, :n], in1=t_[:, :n], op=ALU.mult), reads=[k_t], writes=[t_hid[mc]])
        emit_linear(kb, lr, [(mg[:, k, :n], t_mg[k], 128) for k in range(16)], w1, DFF, n, ev_m1)

        def ev_m2(mc, m, pt, t_p):
            kb.op("dve", lambda g: g.scalar_tensor_tensor(out=x[:, mc, :n], in0=pt[:, :n], scalar=mv[:, mc, seg, 3:4], in1=x[:, mc, :n], op0=ALU.mult, op1=ALU.add),
                  reads=[t_p, t_mv, t_x[mc]], writes=[t_x[mc]])
            kb.dma("pool", outT[mc * 128:(mc + 1) * 128, cs], x[:, mc, :n], reads=[t_x[mc]])
        emit_linear(kb, lr, [(hid[:, k, :n], t_hid[k], 128) for k in range(64)], w2, D, n, ev_m2)
    return kb.finish()


NKT = TT // 128


def build_mla():
    kb = KB()
    cq = kb.dram("cq", [512, TT])
    ckv = kb.dram("ckv", [512, TT])
    krT = kb.dram("krT", [64, TT])
    krsT = kb.dram("krsT", [64, TT])
    cos2 = kb.dram("cos2", [64, TT])
    sin2 = kb.dram("sin2", [64, TT])
    gl = kb.dram("gl", [128, 4, 2])
    gn = kb.dram("gn", [128, 2])
    gr = kb.dram("gr", [64, 4])
    wqn = kb.dram("wqn", [512, 512])
    wqr = kb.dram("wqr", [512, 256])
    wqs = kb.dram("wqs", [512, 256])
    wkn = kb.dram("wkn", [512, 512])
    wv = kb.dram("wv", [512, 512])
    oaT = kb.dram("oaT", [512, TT], kind="ExternalOutput")
    cst = Consts(kb)
    pS = Ring([kb.ps() for _ in range(3)])
    pO = Ring([kb.ps() for _ in range(2)])
    pR = Ring([kb.ps() for _ in range(2)])
    pX = Ring([kb.ps() for _ in range(1)])
    glt = kb.sb([128, 4, 2]); gnt = kb.sb([128, 2]); grt = kb.sb([64, 4]); t_g = Tok()
    kb.dma("sp", glt[:], gl, writes=[t_g]); kb.dma("sp", gnt[:], gn, writes=[t_g]); kb.dma("sp", grt[:], gr, writes=[t_g])
    csring = Ring([kb.sb([64, 2, 512]) for _ in range(2)])

    def load_cs(c0, cn):
        t, tk = csring.next()
        kb.dma("act", t[:, 0, :cn], cos2[:, c0:c0 + cn], writes=[tk])
        kb.dma("act", t[:, 1, :cn], sin2[:, c0:c0 + cn], writes=[tk])
        return t, tk
    wts = {}
    wstage = kb.sb([128, 4, 512]); t_ws = Tok()
    for nm, wd, nc_ in (("wqn", wqn, 512), ("wqr", wqr, 256), ("wqs", wqs, 256), ("wkn", wkn, 512), ("wv", wv, 512)):
        kb.dma("sp", wstage[:, :, :nc_], wd.rearrange("(kc p) n -> p kc n", p=128), writes=[t_ws])
        wb = kb.sb([128, 4, nc_], BF16); tk = Tok()
        kb.op("pool", lambda g: g.tensor_copy(out=wb[:], in_=wstage[:, :, :nc_]), reads=[t_ws], writes=[tk])
        wts[nm] = (wb, tk)
    blks = blocks(TT)
    xin = Ring([kb.sb([128, 4, 512]) for _ in range(2)])
    sqr = Ring([kb.sb([128, 512]) for _ in range(3)])
    tmpr = Ring([kb.sb([128, 512]) for _ in range(4)])
    lat = {}
    for li, (src, nm) in enumerate(((cq, "cqn"), (ckv, "ckvn"))):
        dst = kb.sb([128, 4, TT], BF16); t_d = Tok()
        sv = src.rearrange("(kc p) t -> p kc t", p=128)
        for (c0, cn) in blks:
            xb, t_xb = xin.next()
            kb.dma("sp", xb[:, :, :cn], sv[:, :, c0:c0 + cn], writes=[t_xb])
            pt, t_p = pX.next()
            for kc in range(4):
                sq, k_sq = sqr.next()
                kb.op("act", lambda g: g.activation(out=sq[:, :cn], in_=xb[:, kc, :cn], func=AF.Square), reads=[t_xb], writes=[k_sq])
                kb.op("pe", lambda g: g.matmul(pt[:, :cn], cst.ones[:], sq[:, :cn], start=(kc == 0), stop=(kc == 3)), reads=[k_sq, cst.t_ones], writes=[t_p])
            r_, k_r = tmpr.next()
            kb.op("dve", lambda g: g.tensor_scalar(out=r_[:, :cn], in0=pt[:, :cn], scalar1=1.0 / 512, scalar2=EPS, op0=ALU.mult, op1=ALU.add), reads=[t_p], writes=[k_r])
            kb.op("act", lambda g: g.activation(out=r_[:, :cn], in_=r_[:, :cn], func=AF.Sqrt), reads=[k_r], writes=[k_r])
            kb.op("dve", lambda g: g.reciprocal(out=r_[:, :cn], in_=r_[:, :cn]), reads=[k_r], writes=[k_r])
            for kc in range(4):
                kb.op("dve", lambda g: g.tensor_tensor(out=xb[:, kc, :cn], in0=xb[:, kc, :cn], in1=r_[:, :cn], op=ALU.mult), reads=[t_xb, k_r], writes=[t_xb])
                kb.op("act", lambda g: g.activation(out=dst[:, kc, c0:c0 + cn], in_=xb[:, kc, :cn], func=AF.Copy, scale=glt[:, kc, li:li + 1]), reads=[t_xb, t_g], writes=[t_d])
        lat[nm] = (dst, t_d)
    cqn, t_cqn = lat["cqn"]
    ckvn, t_ckvn = lat["ckvn"]
    krn = kb.sb([64, TT], BF16); t_krn = Tok()
    for (c0, cn) in blks:
        xb, t_xb = xin.next()
        kb.dma("sp", xb[:64, 0, :cn], krT[:, c0:c0 + cn], writes=[t_xb])
        kb.dma("sp", xb[:64, 1, :cn], krsT[:, c0:c0 + cn], writes=[t_xb])
        cs_t, t_cs = load_cs(c0, cn)
        _rope_q(kb, cst, sqr, pX, tmpr, grt, 2, t_g, cs_t, t_cs, xb[:64, 0, :cn], t_xb, xb[:64, 1, :cn], t_xb, c0, cn, krn, t_krn)

    knT = kb.sb([128, TT], BF16); t_kn = Tok()
    qnT = kb.sb([128, TT], BF16); t_qn = Tok()
    qrT = kb.sb([64, TT], BF16); t_qr = Tok()
    V = kb.sb([128, NKT, 128], BF16); t_V = Tok()
    pring = Ring([kb.sb([128, 512], BF16) for _ in range(3)])
    oring = Ring([kb.sb([128, 512]) for _ in range(2)])
    rring = Ring([kb.sb([128, 512]) for _ in range(2)])
    SC = 1.0 / math.sqrt(192.0)
    for h in range(4):
        for (c0, cn) in blks:
            for (wname, srcT, t_srcT, gcol, dstT, t_dst) in (("wkn", ckvn, t_ckvn, 1, knT, t_kn), ("wqn", cqn, t_cqn, 0, qnT, t_qn)):
                wb, t_wb = wts[wname]
                pt, t_p = pS.next()
                for kc in range(4):
                    kb.op("pe", lambda g: g.matmul(pt[:, :cn], wb[:, kc, h * 128:(h + 1) * 128], srcT[:, kc, c0:c0 + cn], start=(kc == 0), stop=(kc == 3)),
                          reads=[t_wb, t_srcT], writes=[t_p])
                sq, k_sq = sqr.next()
                kb.op("act", lambda g: g.activation(out=sq[:, :cn], in_=pt[:, :cn], func=AF.Square), reads=[t_p], writes=[k_sq])
                p2, t_p2 = pX.next()
                kb.op("pe", lambda g: g.matmul(p2[:, :cn], cst.ones[:], sq[:, :cn], start=True, stop=True), reads=[k_sq, cst.t_ones], writes=[t_p2])
                r_, k_r = tmpr.next()
                kb.op("dve", lambda g: g.tensor_scalar(out=r_[:, :cn], in0=p2[:, :cn], scalar1=1.0 / 128, scalar2=EPS, op0=ALU.mult, op1=ALU.add), reads=[t_p2], writes=[k_r])
                kb.op("act", lambda g: g.activation(out=r_[:, :cn], in_=r_[:, :cn], func=AF.Sqrt), reads=[k_r], writes=[k_r])
                kb.op("dve", lambda g: g.reciprocal(out=r_[:, :cn], in_=r_[:, :cn]), reads=[k_r], writes=[k_r])
                kb.op("dve", lambda g: g.tensor_tensor(out=r_[:, :cn], in0=pt[:, :cn], in1=r_[:, :cn], op=ALU.mult), reads=[t_p, k_r], writes=[k_r])
                kb.op("act", lambda g: g.activation(out=dstT[:, c0:c0 + cn], in_=r_[:, :cn], func=AF.Copy, scale=gnt[:, gcol:gcol + 1]), reads=[k_r, t_g], writes=[t_dst])
            wr, t_wr = wts["wqr"]; wsw, t_wsw = wts["wqs"]
            pa, t_pa = pS.next(); pb, t_pb = pS.next()
            for kc in range(4):
                kb.op("pe", lambda g: g.matmul(pa[:64, :cn], wr[:, kc, h * 64:(h + 1) * 64], cqn[:, kc, c0:c0 + cn], start=(kc == 0), stop=(kc == 3)), reads=[t_wr, t_cqn], writes=[t_pa])
            for kc in range(4):
                kb.op("pe", lambda g: g.matmul(pb[:64, :cn], wsw[:, kc, h * 64:(h + 1) * 64], cqn[:, kc, c0:c0 + cn], start=(kc == 0), stop=(kc == 3)), reads=[t_wsw, t_cqn], writes=[t_pb])
            qa, k_qa = tmpr.next()
            kb.op("act", lambda g: g.activation(out=qa[:64, :cn], in_=pa[:64, :cn], func=AF.Copy), reads=[t_pa], writes=[k_qa])
            qb_, k_qb = tmpr.next()
            kb.op("act", lambda g: g.activation(out=qb_[:64, :cn], in_=pb[:64, :cn], func=AF.Copy), reads=[t_pb], writes=[k_qb])
            cs_t, t_cs = load_cs(c0, cn)
            _rope_q(kb, cst, sqr, pX, tmpr, grt, 0, t_g, cs_t, t_cs, qa[:64, :cn], k_qa, qb_[:64, :cn], k_qb, c0, cn, qrT, t_qr)
        wvb, t_wvb = wts["wv"]
        for kt in range(NKT):
            pt, t_p = pS.next()
            for kc in range(4):
                kb.op("pe", lambda g: g.matmul(pt[:, :128], ckvn[:, kc, kt * 128:(kt + 1) * 128], wvb[:, kc, h * 128:(h + 1) * 128], start=(kc == 0), stop=(kc == 3)),
                      reads=[t_ckvn, t_wvb], writes=[t_p])
            e = "act" if kt % 2 == 0 else "dve"
            if e == "act":
                kb.op("act", lambda g: g.activation(out=V[:, kt, :], in_=pt[:, :128], func=AF.Copy), reads=[t_p], writes=[t_V])
            else:
                kb.op("dve", lambda g: g.tensor_copy(out=V[:, kt, :], in_=pt[:, :128]), reads=[t_p], writes=[t_V])
        qblocks = [(0, 256, list(range(2)))] + [(256 + 512 * i, 512, list(range(NKT))) for i in range(8)]
        for (q0, nq, kts) in qblocks:
            po, t_po = pO.next()
            pr, t_pr = pR.next()
            for ki, kt in enumerate(kts):
                ps_, t_ps = pS.next()
                kb.op("pe", lambda g: g.matmul(ps_[:, :nq], knT[:, kt * 128:(kt + 1) * 128], qnT[:, q0:q0 + nq], start=True, stop=False), reads=[t_kn, t_qn], writes=[t_ps])
                kb.op("pe", lambda g: g.matmul(ps_[:, :nq], krn[:, kt * 128:(kt + 1) * 128], qrT[:, q0:q0 + nq], start=False, stop=True), reads=[t_krn, t_qr], writes=[t_ps])
                pT, t_pT = pring.next()
                kb.op("act", lambda g: g.activation(out=pT[:, :nq], in_=ps_[:, :nq], func=AF.Exp, scale=SC), reads=[t_ps], writes=[t_pT])
                kb.op("pe", lambda g: g.matmul(po[:, :nq], V[:, kt, :], pT[:, :nq], start=(ki == 0), stop=(ki == len(kts) - 1)), reads=[t_V, t_pT], writes=[t_po])
                kb.op("pe", lambda g: g.matmul(pr[:, :nq], cst.ones_bf[:], pT[:, :nq], start=(ki == 0), stop=(ki == len(kts) - 1)), reads=[cst.t_ones_bf, t_pT], writes=[t_pr])
            ri, k_ri = rring.next()
            kb.op("dve", lambda g: g.reciprocal(out=ri[:, :nq], in_=pr[:, :nq]), reads=[t_pr], writes=[k_ri])
            ot, k_ot = oring.next()
            kb.op("dve", lambda g: g.tensor_tensor(out=ot[:, :nq], in0=po[:, :nq], in1=ri[:, :nq], op=ALU.mult), reads=[t_po, k_ri], writes=[k_ot])
            kb.dma("pool", oaT[h * 128:(h + 1) * 128, q0:q0 + nq], ot[:, :nq], reads=[k_ot])
    return kb.finish()


def _rope_q(kb, cst, sqr, pX, tmpr, grt, gcol, t_g, cs_t, t_cs, qa, k_qa, qb_, k_qb, c0, cn, outT, t_out):
    sq, k_sq = sqr.next()
    kb.op("act", lambda g: g.activation(out=sq[:64, :cn], in_=qa, func=AF.Square), reads=[k_qa], writes=[k_sq])
    pt, t_p = pX.next()
    kb.op("pe", lambda g: g.matmul(pt[:64, :cn], cst.ones[:64, :64], sq[:64, :cn], start=True, stop=True), reads=[k_sq, cst.t_ones], writes=[t_p])
    r_, k_r = tmpr.next()
    kb.op("dve", lambda g: g.tensor_scalar(out=r_[:64, :cn], in0=pt[:64, :cn], scalar1=1.0 / 64, scalar2=EPS, op0=ALU.mult, op1=ALU.add), reads=[t_p], writes=[k_r])
    kb.op("act", lambda g: g.activation(out=r_[:64, :cn], in_=r_[:64, :cn], func=AF.Sqrt), reads=[k_r], writes=[k_r])
    kb.op("dve", lambda g: g.reciprocal(out=r_[:64, :cn], in_=r_[:64, :cn]), reads=[k_r], writes=[k_r])
    kb.op("dve", lambda g: g.tensor_tensor(out=qa, in0=qa, in1=r_[:64, :cn], op=ALU.mult), reads=[k_qa, k_r], writes=[k_qa])
    kb.op("dve", lambda g: g.scalar_tensor_tensor(out=qa, in0=qa, scalar=grt[:, gcol:gcol + 1], in1=cs_t[:, 0, :cn], op0=ALU.mult, op1=ALU.mult),
          reads=[k_qa, t_g, t_cs], writes=[k_qa])
    kb.op("dve", lambda g: g.tensor_tensor(out=qb_, in0=qb_, in1=r_[:64, :cn], op=ALU.mult), reads=[k_qb, k_r], writes=[k_qb])
    kb.op("dve", lambda g: g.scalar_tensor_tensor(out=qb_, in0=qb_, scalar=grt[:, gcol + 1:gcol + 2], in1=cs_t[:, 1, :cn], op0=ALU.mult, op1=ALU.mult),
          reads=[k_qb, t_g, t_cs], writes=[k_qb])
    kb.op("pool", lambda g: g.tensor_tensor(out=outT[:, c0:c0 + cn], in0=qa, in1=qb_, op=ALU.add), reads=[k_qa, k_qb], writes=[t_out])


Z_OFF = {"cq": 0, "ckv": 512, "kr": 1024, "r": 1088, "k": 2112, "v": 3136, "wdf": 4160, "wdb": 4224, "adf": 4288, "adb": 4352,
         "gd": 4416, "u": 4576, "gate": 5600}


def rope_tables():
    rows = SEQ // 64
    row = np.repeat(np.arange(rows, dtype=np.float32), 64)
    col = np.tile(np.arange(64, dtype=np.float32), rows)
    inv = (np.float32(10000.0) ** (-np.arange(16, dtype=np.float32) / np.float32(16))).astype(np.float32)
    ang = np.concatenate([row[:, None] * inv, col[:, None] * inv], -1).astype(np.float32)
    c = np.cos(ang).astype(np.float32).T
    s = np.sin(ang).astype(np.float32).T
    cos2 = np.ones((64, TT), np.float32)
    sin2 = np.zeros((64, TT), np.float32)
    cos2[:32, CTX:] = c; cos2[32:, CTX:] = c
    sin2[:32, CTX:] = -s; sin2[32:, CTX:] = s
    return cos2, sin2


def sw64(a, axis=0):
    a1, a2 = np.split(a, 2, axis=axis)
    return np.concatenate([a2, a1], axis=axis)


def mla_inputs(zT, P, l, hg, tabs):
    cos2, sin2 = tabs
    hs = range(4 * hg, 4 * hg + 4)
    wuq = P["mla_w_uq"][l].reshape(512, 8, 192)
    wukv = P["mla_w_ukv"][l].reshape(512, 8, 256)
    gqr = P["mla_qn_rope_g"][l]; gkr = P["mla_kn_rope_g"][l]
    c = np.ascontiguousarray
    return {
        "cq": c(zT[0:512]), "ckv": c(zT[512:1024]), "krT": c(zT[1024:1088]), "krsT": c(sw64(zT[1024:1088])),
        "cos2": cos2, "sin2": sin2,
        "gl": c(np.stack([fm(P["mla_q_lora_g"][l]), fm(P["mla_kv_lora_g"][l])], -1)),
        "gn": c(np.stack([P["mla_qn_nope_g"][l], P["mla_kn_nope_g"][l]], -1)),
        "gr": c(np.stack([gqr, sw64(gqr), gkr, sw64(gkr)], -1)),
        "wqn": c(wuq[:, hs, :128].reshape(512, 512)),
        "wqr": c(wuq[:, hs, 128:].reshape(512, 256)),
        "wqs": c(sw64(wuq[:, hs, 128:], axis=2).reshape(512, 256)),
        "wkn": c(wukv[:, hs, :128].reshape(512, 512)),
        "wv": c(wukv[:, hs, 128:].reshape(512, 512)),
    }


NTILE = 32


def build_s5(dbg=False):
    kb = KB(auto_fence=False)
    uT = kb.dram("uT", [1024, TT])
    lam = kb.dram("lam", [128, 3, NTILE])
    bpad = kb.dram("bpad", [128, 2, NTILE, 32])
    cpad = kb.dram("cpad", [128, 2, NTILE, 32])
    ident = kb.dram("ident", [128, 128])
    ysT = kb.dram("ysT", [1024, TT], kind="ExternalOutput")
    pX = Ring([kb.ps() for _ in range(4)])
    T = TT
    lm = kb.sb([128, 3, NTILE]); t_lm = Tok()
    bp = kb.sb([128, 2, NTILE, 32]); t_bp = Tok()
    cp = kb.sb([128, 2, NTILE, 32]); t_cp = Tok()
    idt = kb.sb([128, 128]); t_id = Tok()
    kb.dma("sp", lm[:], lam, writes=[t_lm]); kb.dma("sp", bp[:], bpad, writes=[t_bp]); kb.dma("sp", cp[:], cpad, writes=[t_cp])
    kb.dma("sp", idt[:], ident, writes=[t_id])
    sc = kb.sb([128, 16, NTILE]); t_sc = Tok()
    DT, MAG, TH, CS, SN, T1, T2, ARE, AIM, QRE, QIM, NQIM, RDEN, AM1 = range(14)
    lre, lim, ldt = lm[:, 0, :], lm[:, 1, :], lm[:, 2, :]
    R, W = [t_lm, t_sc], [t_sc]

    def dv(fn):
        kb.op("dve", fn, reads=R, writes=W, fence=True)

    def ac(fn):
        kb.op("act", fn, reads=R, writes=W, fence=True)
    ac(lambda g: g.activation(out=sc[:, DT, :], in_=ldt, func=AF.Exp))
    dv(lambda g: g.tensor_tensor(out=sc[:, MAG, :], in0=lre, in1=sc[:, DT, :], op=ALU.mult))
    ac(lambda g: g.activation(out=sc[:, MAG, :], in_=sc[:, MAG, :], func=AF.Exp))
    dv(lambda g: g.tensor_tensor(out=sc[:, TH, :], in0=lim, in1=sc[:, DT, :], op=ALU.mult))
    ac(lambda g: g.activation(out=sc[:, T1, :], in_=sc[:, TH, :], func=AF.Sin, scale=1.0 / 32))
    dv(lambda g: g.tensor_tensor(out=sc[:, T1, :], in0=sc[:, T1, :], in1=sc[:, T1, :], op=ALU.mult))
    dv(lambda g: g.tensor_scalar(out=sc[:, CS, :], in0=sc[:, T1, :], scalar1=-2.0, scalar2=1.0, op0=ALU.mult, op1=ALU.add))
    ac(lambda g: g.activation(out=sc[:, SN, :], in_=sc[:, TH, :], func=AF.Sin, scale=1.0 / 16))
    for _ in range(4):
        dv(lambda g: g.tensor_tensor(out=sc[:, T1, :], in0=sc[:, CS, :], in1=sc[:, CS, :], op=ALU.mult))
        dv(lambda g: g.tensor_tensor(out=sc[:, T2, :], in0=sc[:, SN, :], in1=sc[:, SN, :], op=ALU.mult))
        dv(lambda g: g.scalar_tensor_tensor(out=sc[:, SN, :], in0=sc[:, CS, :], scalar=2.0, in1=sc[:, SN, :], op0=ALU.mult, op1=ALU.mult))
        dv(lambda g: g.tensor_tensor(out=sc[:, CS, :], in0=sc[:, T1, :], in1=sc[:, T2, :], op=ALU.subtract))
    dv(lambda g: g.tensor_tensor(out=sc[:, ARE, :], in0=sc[:, MAG, :], in1=sc[:, CS, :], op=ALU.mult))
    dv(lambda g: g.tensor_tensor(out=sc[:, AIM, :], in0=sc[:, MAG, :], in1=sc[:, SN, :], op=ALU.mult))
    dv(lambda g: g.tensor_tensor(out=sc[:, T1, :], in0=lre, in1=lre, op=ALU.mult))
    dv(lambda g: g.tensor_tensor(out=sc[:, T2, :], in0=lim, in1=lim, op=ALU.mult))
    dv(lambda g: g.tensor_tensor(out=sc[:, RDEN, :], in0=sc[:, T1, :], in1=sc[:, T2, :], op=ALU.add))
    dv(lambda g: g.reciprocal(out=sc[:, RDEN, :], in_=sc[:, RDEN, :]))
    dv(lambda g: g.tensor_scalar(out=sc[:, AM1, :], in0=sc[:, ARE, :], scalar1=-1.0, scalar2=None, op0=ALU.add))
    dv(lambda g: g.tensor_tensor(out=sc[:, T1, :], in0=sc[:, AM1, :], in1=lre, op=ALU.mult))
    dv(lambda g: g.tensor_tensor(out=sc[:, T2, :], in0=sc[:, AIM, :], in1=lim, op=ALU.mult))
    dv(lambda g: g.tensor_tensor(out=sc[:, QRE, :], in0=sc[:, T1, :], in1=sc[:, T2, :], op=ALU.add))
    dv(lambda g: g.tensor_tensor(out=sc[:, QRE, :], in0=sc[:, QRE, :], in1=sc[:, RDEN, :], op=ALU.mult))
    dv(lambda g: g.tensor_tensor(out=sc[:, T1, :], in0=sc[:, AIM, :], in1=lre, op=ALU.mult))
    dv(lambda g: g.tensor_tensor(out=sc[:, T2, :], in0=sc[:, AM1, :], in1=lim, op=ALU.mult))
    dv(lambda g: g.tensor_tensor(out=sc[:, QIM, :], in0=sc[:, T1, :], in1=sc[:, T2, :], op=ALU.subtract))
    dv(lambda g: g.tensor_tensor(out=sc[:, QIM, :], in0=sc[:, QIM, :], in1=sc[:, RDEN, :], op=ALU.mult))
    dv(lambda g: g.tensor_scalar(out=sc[:, NQIM, :], in0=sc[:, QIM, :], scalar1=-1.0, scalar2=None, op0=ALU.mult))
    bb = kb.sb([128, 2, NTILE, 32]); t_bb = Tok()
    bsc = kb.sb([128, 32]); t_bsc = Tok()
    for j in range(NTILE):
        for ri, (x0, x1, qq) in enumerate(((0, 1, NQIM), (1, 0, QIM))):
            kb.op("pool", lambda g: g.tensor_scalar(out=bb[:, ri, j, :], in0=bp[:, x0, j, :], scalar1=sc[:, QRE, j:j + 1], scalar2=None, op0=ALU.mult), reads=[t_bp, t_sc], writes=[t_bb])
            kb.op("pool", lambda g: g.tensor_scalar(out=bsc[:], in0=bp[:, x1, j, :], scalar1=sc[:, qq, j:j + 1], scalar2=None, op0=ALU.mult), reads=[t_bp, t_sc], writes=[t_bsc])
            kb.op("pool", lambda g: g.tensor_tensor(out=bb[:, ri, j, :], in0=bb[:, ri, j, :], in1=bsc[:], op=ALU.add), reads=[t_bsc, t_bb], writes=[t_bb])
    NLV = 13
    ES = kb.sb([128, NLV, 3, NTILE]); t_ES = Tok()
    esc = kb.sb([128, 3, NTILE]); t_esc = Tok()
    RE_, WE_ = [t_sc, t_ES, t_esc], [t_ES, t_esc]
    kb.op("pool", lambda g: g.tensor_copy(out=ES[:, 0, 0, :], in_=sc[:, CS, :]), reads=RE_, writes=WE_)
    kb.op("pool", lambda g: g.tensor_copy(out=ES[:, 0, 1, :], in_=sc[:, SN, :]), reads=RE_, writes=WE_)
    for lv in range(NLV):
        if lv > 0:
            kb.op("pool", lambda g: g.tensor_tensor(out=esc[:, 0, :], in0=ES[:, lv - 1, 0, :], in1=ES[:, lv - 1, 0, :], op=ALU.mult), reads=RE_, writes=WE_)
            kb.op("pool", lambda g: g.tensor_tensor(out=esc[:, 1, :], in0=ES[:, lv - 1, 1, :], in1=ES[:, lv - 1, 1, :], op=ALU.mult), reads=RE_, writes=WE_)
            kb.op("pool", lambda g: g.tensor_tensor(out=esc[:, 2, :], in0=ES[:, lv - 1, 0, :], in1=ES[:, lv - 1, 1, :], op=ALU.mult), reads=RE_, writes=WE_)
            kb.op("pool", lambda g: g.tensor_tensor(out=ES[:, lv, 0, :], in0=esc[:, 0, :], in1=esc[:, 1, :], op=ALU.subtract), reads=RE_, writes=WE_)
            kb.op("pool", lambda g: g.tensor_scalar(out=ES[:, lv, 1, :], in0=esc[:, 2, :], scalar1=2.0, scalar2=None, op0=ALU.mult), reads=RE_, writes=WE_)
        kb.op("pool", lambda g: g.tensor_tensor(out=esc[:, 0, :], in0=ES[:, lv, 0, :], in1=ES[:, lv, 0, :], op=ALU.mult), reads=RE_, writes=WE_)
        kb.op("pool", lambda g: g.tensor_tensor(out=esc[:, 1, :], in0=ES[:, lv, 1, :], in1=ES[:, lv, 1, :], op=ALU.mult), reads=RE_, writes=WE_)
        kb.op("pool", lambda g: g.tensor_tensor(out=esc[:, 0, :], in0=esc[:, 0, :], in1=esc[:, 1, :], op=ALU.add), reads=RE_, writes=WE_)
        kb.op("pool", lambda g: g.tensor_scalar(out=esc[:, 0, :], in0=esc[:, 0, :], scalar1=-0.5, scalar2=1.5, op0=ALU.mult, op1=ALU.add), reads=RE_, writes=WE_)
        kb.op("pool", lambda g: g.tensor_tensor(out=ES[:, lv, 0, :], in0=ES[:, lv, 0, :], in1=esc[:, 0, :], op=ALU.mult), reads=RE_, writes=WE_)
        kb.op("pool", lambda g: g.tensor_tensor(out=ES[:, lv, 1, :], in0=ES[:, lv, 1, :], in1=esc[:, 0, :], op=ALU.mult), reads=RE_, writes=WE_)
        kb.op("pool", lambda g: g.tensor_scalar(out=ES[:, lv, 2, :], in0=ES[:, lv, 1, :], scalar1=-1.0, scalar2=None, op0=ALU.mult), reads=RE_, writes=WE_)
    Ct = kb.sb([128, T]); St = kb.sb([128, T]); t_tab = Tok()
    bre = kb.sb([128, T]); bim = kb.sb([128, T]); t_bu = Tok()
    gre = kb.sb([128, T]); gim = kb.sb([128, T]); t_gg = Tok()
    Rt = kb.sb([128, T]); t_R = Tok()
    tA = kb.sb([128, T]); t_tA = Tok()
    tB = kb.sb([128, T]); t_tB = Tok()
    ones_T = tA
    uring = Ring([kb.sb([32, 512]) for _ in range(3)])
    oring = Ring([kb.sb([32, 512]) for _ in range(3)])
    bbT = kb.sb([32, 2, 128]); t_bbT = Tok()
    blks = blocks(T)
    for j in range(NTILE):
        for ri in range(2):
            pt, t_p = pX.next()
            kb.op("pe", lambda g: g.transpose(out=pt[:32, :128], in_=bb[:, ri, j, :], identity=idt[:]), reads=[t_bb, t_id], writes=[t_p])
            kb.op("act", lambda g: g.activation(out=bbT[:, ri, :], in_=pt[:32, :128], func=AF.Copy), reads=[t_p], writes=[t_bbT], fence=True)
        for (c0, cn) in blks:
            ut, t_u = uring.next()
            kb.dma("sp", ut[:, :cn], uT[32 * j:32 * j + 32, c0:c0 + cn], writes=[t_u])
            for ri, dst in enumerate((bre, bim)):
                pt, t_p = pX.next()
                kb.op("pe", lambda g: g.matmul(pt[:, :cn], bbT[:, ri, :], ut[:, :cn], start=True, stop=True), reads=[t_bbT, t_u], writes=[t_p])
                kb.op("act", lambda g: g.activation(out=dst[:, c0:c0 + cn], in_=pt[:, :cn], func=AF.Copy), reads=[t_p], writes=[t_bu])
        kb.op("pool", lambda g: g.memset(Rt[:], 1.0), writes=[t_R])
        kb.op("act", lambda g: g.activation(out=Rt[:], in_=Rt[:], func=AF.Copy, scale=sc[:, MAG, j:j + 1]), reads=[t_R, t_sc], writes=[t_R])
        kb.op("pool", lambda g: g.memset(Ct[:, 0:1], 1.0), writes=[t_tab])
        kb.op("pool", lambda g: g.memset(St[:, 0:1], 0.0), writes=[t_tab])
        m = 1
        lv = 0
        while m < T:
            w_ = min(m, T - m)
            RT, WT = [t_tab, t_ES, t_tA, t_tB], [t_tab, t_tA, t_tB]
            ec, es_, en = ES[:, lv, 0, j:j + 1], ES[:, lv, 1, j:j + 1], ES[:, lv, 2, j:j + 1]
            kb.op("dve", lambda g: g.tensor_scalar(out=tA[:, :w_], in0=Ct[:, :w_], scalar1=ec, scalar2=None, op0=ALU.mult), reads=RT, writes=WT)
            kb.op("dve", lambda g: g.scalar_tensor_tensor(out=Ct[:, m:m + w_], in0=St[:, :w_], scalar=en, in1=tA[:, :w_], op0=ALU.mult, op1=ALU.add), reads=RT, writes=WT)
            kb.op("dve", lambda g: g.tensor_scalar(out=tB[:, :w_], in0=St[:, :w_], scalar1=ec, scalar2=None, op0=ALU.mult), reads=RT, writes=WT)
            kb.op("dve", lambda g: g.scalar_tensor_tensor(out=St[:, m:m + w_], in0=Ct[:, :w_], scalar=es_, in1=tB[:, :w_], op0=ALU.mult, op1=ALU.add), reads=RT, writes=WT)
            m *= 2
            lv += 1
        kb.op("pool", lambda g: g.tensor_tensor(out=tA[:], in0=bre[:], in1=Ct[:], op=ALU.mult), reads=[t_bu, t_tab, t_tA], writes=[t_tA])
        kb.op("dve", lambda g: g.tensor_tensor(out=tB[:], in0=bim[:], in1=St[:], op=ALU.mult), reads=[t_bu, t_tab, t_tB], writes=[t_tB])
        kb.op("dve", lambda g: g.tensor_tensor(out=tA[:], in0=tA[:], in1=tB[:], op=ALU.add), reads=[t_tA, t_tB], writes=[t_tA])
        kb.op("dve", lambda g: g.tensor_tensor_scan(out=gre[:], data0=Rt[:], data1=tA[:], initial=0.0, op0=ALU.mult, op1=ALU.add), reads=[t_R, t_tA, t_gg], writes=[t_gg])
        kb.op("pool", lambda g: g.tensor_tensor(out=tA[:], in0=bim[:], in1=Ct[:], op=ALU.mult), reads=[t_bu, t_tab, t_tA], writes=[t_tA])
        kb.op("dve", lambda g: g.tensor_tensor(out=tB[:], in0=bre[:], in1=St[:], op=ALU.mult), reads=[t_bu, t_tab, t_tB], writes=[t_tB])
        kb.op("dve", lambda g: g.tensor_tensor(out=tA[:], in0=tA[:], in1=tB[:], op=ALU.subtract), reads=[t_tA, t_tB], writes=[t_tA])
        kb.op("dve", lambda g: g.tensor_tensor_scan(out=gim[:], data0=Rt[:], data1=tA[:], initial=0.0, op0=ALU.mult, op1=ALU.add), reads=[t_R, t_tA, t_gg], writes=[t_gg])
        kb.op("pool", lambda g: g.tensor_tensor(out=tA[:], in0=gre[:], in1=Ct[:], op=ALU.mult), reads=[t_gg, t_tab, t_tA], writes=[t_tA])
        kb.op("dve", lambda g: g.tensor_tensor(out=tB[:], in0=gim[:], in1=St[:], op=ALU.mult), reads=[t_gg, t_tab, t_tB], writes=[t_tB])
        kb.op("dve", lambda g: g.tensor_tensor(out=bre[:], in0=tA[:], in1=tB[:], op=ALU.subtract), reads=[t_tA, t_tB, t_bu], writes=[t_bu])
        kb.op("pool", lambda g: g.tensor_tensor(out=tA[:], in0=gre[:], in1=St[:], op=ALU.mult), reads=[t_gg, t_tab, t_tA], writes=[t_tA])
        kb.op("dve", lambda g: g.tensor_tensor(out=tB[:], in0=gim[:], in1=Ct[:], op=ALU.mult), reads=[t_gg, t_tab, t_tB], writes=[t_tB])
        kb.op("dve", lambda g: g.scalar_tensor_tensor(out=bim[:], in0=tA[:], scalar=-1.0, in1=tB[:], op0=ALU.mult, op1=ALU.subtract), reads=[t_tA, t_tB, t_bu], writes=[t_bu])
        for (c0, cn) in blks:
            ot, t_o = oring.next()
            pt, t_p = pX.next()
            kb.op("pe", lambda g: g.matmul(pt[:32, :cn], cp[:, 0, j, :], bre[:, c0:c0 + cn], start=True, stop=False), reads=[t_cp, t_bu], writes=[t_p])
            kb.op("pe", lambda g: g.matmul(pt[:32, :cn], cp[:, 1, j, :], bim[:, c0:c0 + cn], start=False, stop=True), reads=[t_cp, t_bu], writes=[t_p])
            kb.op("act", lambda g: g.activation(out=ot[:, :cn], in_=pt[:32, :cn], func=AF.Copy), reads=[t_p], writes=[t_o])
            kb.dma("sp", ysT[32 * j:32 * j + 32, c0:c0 + cn], ot[:, :cn], reads=[t_o])
    if dbg:
        for nm, t_, tk in (("d_sc", sc, t_sc), ("d_C", Ct, t_tab), ("d_S", St, t_tab), ("d_bre", bre, t_bu), ("d_gre", gre, t_gg), ("d_R", Rt, t_R), ("d_bbT", bbT, t_bbT), ("d_ES", ES, t_ES)):
            dd = kb.dram(nm, list(t_.shape), kind="ExternalOutput")
            kb.dma("sp", dd, t_[:], reads=[tk])
    return kb.finish()


def seg_rev(a):
    return np.concatenate([a[..., :CTX][..., ::-1], a[..., CTX:][..., ::-1]], -1)


def s5_inputs(zT, P, l, d):
    u = zT[Z_OFF["u"]:Z_OFF["u"] + 1024]
    if d == 1:
        u = seg_rev(u)

    def st(a):
        return a.reshape(NTILE, 2, 64).transpose(1, 2, 0).reshape(128, NTILE)
    lam = np.stack([st(P["s5_lam_re"][l, d]), st(P["s5_lam_im"][l, d]), st(np.broadcast_to(P["s5_log_dt"][l, d][:, None], (64, 64)))], 1)

    def pad(a):
        o = np.zeros((2, 64, NTILE, 2, 16), np.float32)
        a4 = a.reshape(NTILE, 2, 64, 16)
        for gl in range(2):
            o[gl, :, :, gl, :] = a4[:, gl].transpose(1, 0, 2)
        return o.reshape(128, NTILE, 32)
    bpad = np.stack([pad(P["s5_b_re"][l, d]), pad(P["s5_b_im"][l, d])], 1)
    cpad = np.stack([pad(P["s5_c_re"][l, d].transpose(0, 2, 1)), pad(P["s5_c_im"][l, d].transpose(0, 2, 1))], 1)
    c = np.ascontiguousarray
    return {"uT": c(u), "lam": c(lam.astype(np.float32)), "bpad": c(bpad), "cpad": c(cpad), "ident": np.eye(128, dtype=np.float32)}


NCH = TT // 64


def build_rwkv(n_chunks=NCH):
    kb = KB()
    rkvT = kb.dram("rkvT", [3072, TT])
    wdT = kb.dram("wdT", [64, TT])
    adT = kb.dram("adT", [64, TT])
    convw = kb.dram("convw", [128, 24, 3])
    vecs = kb.dram("vecs", [128, 8, 4])
    w2 = kb.dram("w2", [64, 1024])
    a2 = kb.dram("a2", [64, 1024])
    bdm = kb.dram("bdm", [128, 128])
    ohv = kb.dram("ohv", [128, 64, 128])
    ohy = kb.dram("ohy", [128, 64, 128])
    ident = kb.dram("ident", [128, 128])
    convT = kb.dram("convT", [3, 1024, TT], kind="ExternalOutput")
    scanT = kb.dram("scanT", [5, 1024, TT], kind="ExternalOutput")
    ytok = kb.dram("ytok", [NCH, 128, 512], kind="ExternalOutput")
    T = TT
    pX = Ring([kb.ps() for _ in range(2)])
    cw = kb.sb([128, 24, 3]); vc_ = kb.sb([128, 8, 4]); t_c = Tok()
    kb.dma("sp", cw[:], convw, writes=[t_c]); kb.dma("sp", vc_[:], vecs, writes=[t_c])
    w2t = kb.sb([64, 1024]); a2t = kb.sb([64, 1024]); t_w2 = Tok()
    kb.dma("sp", w2t[:], w2, writes=[t_w2]); kb.dma("sp", a2t[:], a2, writes=[t_w2])
    bd = kb.sb([128, 128]); t_bd = Tok()
    kb.dma("sp", bd[:], bdm, writes=[t_bd])
    t_dram = Tok()
    with kb.scope():
        twd = kb.sb([64, T]); t_twd = Tok()
        adt = kb.sb([64, T]); t_ad = Tok()
        kb.dma("sp", twd[:], wdT, writes=[t_twd]); kb.dma("sp", adt[:], adT, writes=[t_ad])
        kb.op("act", lambda g: g.activation(out=twd[:], in_=twd[:], func=AF.Tanh), reads=[t_twd], writes=[t_twd])
        xin = Ring([kb.sb([128, T]) for _ in range(2)])
        obuf = Ring([kb.sb([128, T]) for _ in range(3)])
        kc = kb.sb([128, T]); t_kc = Tok()
        kk = kb.sb([128, T]); t_kk = Tok()
        at = kb.sb([128, T]); t_at = Tok()
        t_dram = Tok()
        segs = ((0, CTX), (CTX, T))
        blks = blocks(T)
        NEG_E = -math.exp(-0.5)
        for pc in range(8):
            rs = slice(pc * 128, (pc + 1) * 128)
            for ai in range(3):
                x, t_x = xin.next()
                kb.dma("sp", x[:], rkvT[ai * 1024 + pc * 128: ai * 1024 + (pc + 1) * 128, :], writes=[t_x])
                if ai == 1:
                    o, t_o = kc, t_kc
                else:
                    o, t_o = obuf.next()
                col = ai * 8 + pc
                kb.op("dve", lambda g: g.tensor_scalar(out=o[:], in0=x[:], scalar1=cw[:, col, 1:2], scalar2=None, op0=ALU.mult), reads=[t_x, t_c], writes=[t_o])
                for (s0, s1) in segs:
                    kb.op("dve", lambda g: g.scalar_tensor_tensor(out=o[:, s0 + 1:s1], in0=x[:, s0:s1 - 1], scalar=cw[:, col, 0:1], in1=o[:, s0 + 1:s1], op0=ALU.mult, op1=ALU.add),
                          reads=[t_x, t_c, t_o], writes=[t_o])
                    kb.op("dve", lambda g: g.scalar_tensor_tensor(out=o[:, s0:s1 - 1], in0=x[:, s0 + 1:s1], scalar=cw[:, col, 2:3], in1=o[:, s0:s1 - 1], op0=ALU.mult, op1=ALU.add),
                          reads=[t_x, t_c, t_o], writes=[t_o])
                kb.dma("pool", convT[ai, rs, :], o[:], reads=[t_o], writes=[t_dram])
            kb.op("pool", lambda g: g.tensor_scalar(out=kk[:], in0=kc[:], scalar1=vc_[:, pc, 2:3], scalar2=None, op0=ALU.mult), reads=[t_kc, t_c], writes=[t_kk])
            sq, t_sq = obuf.next()
            kb.op("act", lambda g: g.activation(out=sq[:], in_=kk[:], func=AF.Square), reads=[t_kk], writes=[t_sq])
            rn, t_rn = obuf.next()
            for (c0, cn) in blks:
                pt, t_p = pX.next()
                kb.op("pe", lambda g: g.matmul(pt[:, :cn], bd[:], sq[:, c0:c0 + cn], start=True, stop=True), reads=[t_bd, t_sq], writes=[t_p])
                kb.op("dve", lambda g: g.tensor_scalar(out=rn[:, c0:c0 + cn], in0=pt[:, :cn], scalar1=1e-12, scalar2=None, op0=ALU.add), reads=[t_p], writes=[t_rn])
            kb.op("act", lambda g: g.activation(out=rn[:], in_=rn[:], func=AF.Sqrt), reads=[t_rn], writes=[t_rn])
            kb.op("dve", lambda g: g.reciprocal(out=rn[:], in_=rn[:]), reads=[t_rn], writes=[t_rn])
            kb.op("dve", lambda g: g.tensor_tensor(out=kk[:], in0=kk[:], in1=rn[:], op=ALU.mult), reads=[t_kk, t_rn], writes=[t_kk])
            al, t_al = obuf.next()
            kb.op("act", lambda g: g.activation(out=al[:], in_=kk[:], func=AF.Copy, scale=-1.0), reads=[t_kk], writes=[t_al])
            kb.dma("pool", scanT[0, rs, :], al[:], reads=[t_al], writes=[t_dram])
            wt_, t_wt = obuf.next()
            for (c0, cn) in blks:
                pt, t_p = pX.next()
                kb.op("pe", lambda g: g.matmul(pt[:, :cn], a2t[:, rs], adt[:, c0:c0 + cn], start=True, stop=True), reads=[t_w2, t_ad], writes=[t_p])
                kb.op("act", lambda g: g.activation(out=at[:, c0:c0 + cn], in_=pt[:, :cn], func=AF.Sigmoid, bias=vc_[:, pc, 1:2], scale=1.0), reads=[t_p, t_c], writes=[t_at])
                pt, t_p = pX.next()
                kb.op("pe", lambda g: g.matmul(pt[:, :cn], w2t[:, rs], twd[:, c0:c0 + cn], start=True, stop=True), reads=[t_w2, t_twd], writes=[t_p])
                kb.op("act", lambda g: g.activation(out=wt_[:, c0:c0 + cn], in_=pt[:, :cn], func=AF.Sigmoid, bias=vc_[:, pc, 0:1], scale=1.0), reads=[t_p, t_c], writes=[t_wt])
            kb.op("act", lambda g: g.activation(out=wt_[:], in_=wt_[:], func=AF.Exp, scale=NEG_E), reads=[t_wt], writes=[t_wt])
            kb.dma("pool", scanT[1, rs, :], wt_[:], reads=[t_wt], writes=[t_dram])
            kb.dma("pool", scanT[4, rs, :], at[:], reads=[t_at], writes=[t_dram])
            be, t_be = obuf.next()
            kb.op("pool", lambda g: g.tensor_tensor(out=be[:], in0=kk[:], in1=at[:], op=ALU.mult), reads=[t_kk, t_at], writes=[t_be])
            kb.dma("pool", scanT[2, rs, :], be[:], reads=[t_be], writes=[t_dram])
            kr_, t_kr = obuf.next()
            kb.op("dve", lambda g: g.tensor_scalar(out=kr_[:], in0=at[:], scalar1=-1.0, scalar2=vc_[:, pc, 3:4], op0=ALU.add, op1=ALU.mult), reads=[t_at, t_c], writes=[t_kr])
            kb.op("dve", lambda g: g.scalar_tensor_tensor(out=kr_[:], in0=kr_[:], scalar=1.0, in1=kc[:], op0=ALU.add, op1=ALU.mult), reads=[t_kr, t_kc], writes=[t_kr])
            kb.dma("pool", scanT[3, rs, :], kr_[:], reads=[t_kr], writes=[t_dram])
        kb.barrier(["sp", "act", "pool", "dve", "pe"])
    kb.auto_fence = False
    ohv_r = kb.sb([128, 64, 128], F32R); ohy_r = kb.sb([128, 64, 128], F32R); bd_r = kb.sb([128, 128], F32R); t_k = Tok()
    idt = kb.sb([128, 128]); t_id = Tok()
    kb.dma("sp", idt[:], ident, writes=[t_id])
    stg = kb.sb([128, 4096]); t_kc = Tok()
    for src, dst in ((ohv, ohv_r), (ohy, ohy_r)):
        for hf in range(2):
            kb.dma("sp", stg[:, :4096].rearrange("p (a b) -> p a b", b=128), src[:, hf * 32:(hf + 1) * 32, :], reads=[t_kc], writes=[t_kc])
            kb.op("act", lambda g: g.activation(out=dst[:, hf * 32:(hf + 1) * 32, :], in_=stg[:, :4096].rearrange("p (a b) -> p a b", b=128), func=AF.Copy),
                  reads=[t_kc], writes=[t_k])
    kb.op("act", lambda g: g.activation(out=bd_r[:], in_=bd[:], func=AF.Copy), reads=[t_bd], writes=[t_k])
    H = kb.ps(); t_H = Tok()
    Yp = kb.ps(); t_Y = Tok()
    pP = Ring([kb.ps() for _ in range(2)])
    pV = Ring([kb.ps() for _ in range(2)])
    kb.op("dve", lambda g: g.memset(H[:], 0.0), writes=[t_H])
    si_ring = Ring([kb.sb([128, 5, 8, 64]) for _ in range(2)])
    vf_ring = Ring([kb.sb([128, 8, 64]) for _ in range(2)])
    vbd = kb.sb([128, 8, 128]); t_vbd = Tok()
    kb.op("pool", lambda g: g.memset(vbd[:], 0.0), writes=[t_vbd])
    vtok_ring = Ring([kb.sb([128, 8, 64], F32R) for _ in range(2)])
    tmp_ring = Ring([kb.sb([128, 512], F32R) for _ in range(2)])
    tmp2_ring = Ring([kb.sb([128, 512], F32R) for _ in range(2)])
    t2_ring = Ring([kb.sb([128, 512]) for _ in range(2)])
    vb_ring = Ring([kb.sb([128, 512]) for _ in range(3)])
    u_ring = Ring([kb.sb([128, 512]) for _ in range(3)])
    yo_ring = Ring([kb.sb([128, 512]) for _ in range(2)])
    srcs = [scanT[0], scanT[1], scanT[2], scanT[3], convT[0]]
    H3 = H[:].rearrange("p (a b) -> p a b", b=64)
    for c in range(n_chunks):
        t0 = c * 64
        si, t_si = si_ring.next()
        for k5, sd in enumerate(srcs):
            kb.dma("sp", si[:, k5, :, :], sd[:, t0:t0 + 64].rearrange("(pc p) t -> p pc t", p=128), reads=[t_dram], writes=[t_si])
        vf, t_vf = vf_ring.next()
        kb.dma("sp", vf[:], convT[2][:, t0:t0 + 64].rearrange("(pc p) t -> p pc t", p=128), reads=[t_dram], writes=[t_vf])
        kb.op("pool", lambda g: g.tensor_copy(out=vbd[0:64, :, 0:64], in_=vf[0:64, :, :]), reads=[t_vf, t_vbd], writes=[t_vbd])
        kb.op("pool", lambda g: g.tensor_copy(out=vbd[64:128, :, 64:128], in_=vf[64:128, :, :]), reads=[t_vf, t_vbd], writes=[t_vbd])
        vtok, t_vt = vtok_ring.next()
        for pr in range(8):
            pt, t_p = pV.next()
            kb.op("pe", lambda g: g.transpose(out=pt[:, :128], in_=vbd[:, pr, :], identity=idt[:]), reads=[t_vbd, t_id], writes=[t_p])
            kb.op("act", lambda g: g.activation(out=vtok[0:64, pr, :], in_=pt[0:64, 0:64], func=AF.Copy), reads=[t_p], writes=[t_vt])
            kb.op("act", lambda g: g.activation(out=vtok[64:128, pr, :], in_=pt[64:128, 64:128], func=AF.Copy), reads=[t_p], writes=[t_vt])
        vt2 = vtok[:].rearrange("p a b -> p (a b)")
        for tau in range(64):
            def bc(k5):
                return si[:, k5, :, tau:tau + 1].to_broadcast([128, 8, 64])
            pv, t_pv = pV.next()
            kb.op("pe", lambda g: g.matmul(pv[:], ohv_r[:, tau, :], vt2, start=True, stop=True), reads=[t_k, t_vt], writes=[t_pv])
            vb, t_vb = vb_ring.next()
            kb.op("act", lambda g: g.activation(out=vb[:], in_=pv[:], func=AF.Copy), reads=[t_pv], writes=[t_vb])
            u, t_u = u_ring.next()
            kb.op("pool", lambda g: g.tensor_tensor(out=u[:].rearrange("p (a b) -> p a b", b=64), in0=vb[:].rearrange("p (a b) -> p a b", b=64), in1=bc(3), op=ALU.mult),
                  reads=[t_vb, t_si], writes=[t_u])
            tm, t_tm = tmp_ring.next()
            kb.op("dve", lambda g: g.tensor_tensor(out=tm[:].rearrange("p (a b) -> p a b", b=64), in0=H3, in1=bc(0), op=ALU.mult), reads=[t_H, t_si], writes=[t_tm])
            pp, t_pp = pP.next()
            kb.op("pe", lambda g: g.matmul(pp[:], bd_r[:], tm[:], start=True, stop=True), reads=[t_k, t_tm], writes=[t_pp])
            kb.op("dve", lambda g: g.tensor_tensor(out=H3, in0=H3, in1=bc(1), op=ALU.mult), reads=[t_H, t_si], writes=[t_H])
            t2, t_t2 = t2_ring.next()
            kb.op("dve", lambda g: g.tensor_tensor(out=t2[:].rearrange("p (a b) -> p a b", b=64), in0=pp[:].rearrange("p (a b) -> p a b", b=64), in1=bc(2), op=ALU.mult),
                  reads=[t_pp, t_si], writes=[t_t2])
            kb.op("dve", lambda g: g.tensor_tensor(out=H[:], in0=H[:], in1=t2[:], op=ALU.add), reads=[t_H, t_t2], writes=[t_H])
            kb.op("dve", lambda g: g.tensor_tensor(out=H[:], in0=H[:], in1=u[:], op=ALU.add), reads=[t_H, t_u], writes=[t_H])
            tm2, t_tm2 = tmp2_ring.next()
            kb.op("dve", lambda g: g.tensor_tensor(out=tm2[:].rearrange("p (a b) -> p a b", b=64), in0=H3, in1=bc(4), op=ALU.mult), reads=[t_H, t_si], writes=[t_tm2])
            kb.op("pe", lambda g: g.matmul(Yp[:], ohy_r[:, tau, :], tm2[:], start=(tau == 0), stop=(tau == 63)), reads=[t_k, t_tm2], writes=[t_Y])
        yo, t_yo = yo_ring.next()
        kb.op("act", lambda g: g.activation(out=yo[:], in_=Yp[:], func=AF.Copy), reads=[t_Y], writes=[t_yo])
        kb.dma("pool", ytok[c], yo[:], reads=[t_yo])
    return kb.finish()


def rwkv_consts():
    bd = np.zeros((128, 128), np.float32); bd[:64, :64] = 1; bd[64:, 64:] = 1
    ohv = np.zeros((2, 64, 64, 2, 64), np.float32)
    ohy = np.zeros((2, 64, 64, 2, 64), np.float32)
    for hp in range(2):
        for t in range(64):
            ohv[hp, t, t, hp, :] = 1
            ohy[hp, :, t, hp, t] = 1
    return bd, ohv.reshape(128, 64, 128), ohy.reshape(128, 64, 128), np.eye(128, dtype=np.float32)


def rwkv_inputs(zT, P, l, d, consts):
    bd, ohv, ohy, ident = consts
    c = np.ascontiguousarray
    rkv = zT[Z_OFF["r"]:Z_OFF["r"] + 3072]
    wd = zT[Z_OFF["wdf"] + 64 * d: Z_OFF["wdf"] + 64 * d + 64]
    ad = zT[Z_OFF["adf"] + 64 * d: Z_OFF["adf"] + 64 * d + 64]
    conv = P["rwkv_conv"][l]
    if d == 1:
        rkv, wd, ad = seg_rev(rkv), seg_rev(wd), seg_rev(ad)
        conv = conv[::-1]
    convw = conv.T.reshape(24, 128, 3).transpose(1, 0, 2)
    vecs = np.stack([fm(P["rwkv_w0"][l, d]), fm(P["rwkv_a0"][l, d]), fm(P["rwkv_k_k"][l]), fm(P["rwkv_k_a"][l])], -1)
    return {"rkvT": c(rkv), "wdT": c(wd), "adT": c(ad), "convw": c(convw), "vecs": c(vecs), "w2": c(P["rwkv_w2"][l, d]), "a2": c(P["rwkv_a2"][l, d]),
            "bdm": bd, "ohv": ohv, "ohy": ohy, "ident": ident}


def ytok_to_fm(ytok):
    y = ytok.reshape(NCH, 2, 64, 8, 64)
    return np.ascontiguousarray(y.transpose(3, 1, 4, 0, 2).reshape(1024, TT))


def lc_inputs(P, l, b, half, xT_b, zT_b, oa_b, rw_b, ys_b, mods, consts):
    c = np.ascontiguousarray
    cs = slice(half * TC, (half + 1) * TC)
    va = mods[4, l] if half == 0 else mods[b, l]
    vb = mods[b, l]
    modv = np.stack([np.stack([fm(v[2]), fm(v[3]), fm(v[4]), fm(v[5])], -1) for v in (va, vb)], 2)
    rvec = np.stack([fm(P["rwkv_ln_g"][l]), fm(P["rwkv_ln_b"][l]), fm(P["rwkv_k_a"][l]), fm(P["rwkv_r_k"][l].reshape(-1)),
                     fm(P["s5_d"][l]), fm(P["s5_glu_b"][l])], -1)
    return {
        "xT": c(xT_b[:, cs]), "oaT": c(oa_b[:, cs]), "rw": c(rw_b[:, :, cs]), "gdT": c(zT_b[Z_OFF["gd"]:Z_OFF["gd"] + 160, cs]),
        "s5": c(np.stack([ys_b[0][:, cs], ys_b[1][:, cs], zT_b[Z_OFF["u"]:Z_OFF["u"] + 1024, cs]])),
        "gateT": c(zT_b[Z_OFF["gate"]:, cs]), "modv": c(modv), "g2n": fm(P["norm2_g"][l]), "rvec": c(rvec), "bdm": consts[0],
        "w_g2": c(P["rwkv_g2"][l]), "w_glu": c(P["s5_glu_w"][l]), "w_br": c(P["w_branch"][l].reshape(3072, D)), "w_out": c(P["w_out"][l]),
        "w1": c(P["w_mlp1"][l]), "w2": c(P["w_mlp2"][l]),
    }


def la_inputs(P, l, b, half, xT_b, mods):
    va = mods[4, l] if half == 0 else mods[b, l]
    vb = mods[b, l]
    modv = np.stack([np.stack([fm(v[0]), fm(v[1])], -1) for v in (va, vb)], 2)
    return {"xT": np.ascontiguousarray(xT_b[:, half * TC:(half + 1) * TC]), "modv": np.ascontiguousarray(modv), "g1": fm(P["norm1_g"][l]), "w_in": P["w_in"][l]}


def run_mods(P):
    C5 = np.concatenate([P["c"], P["c_ctx"][None]], 0)
    cv = np.ascontiguousarray(C5.T.reshape(16, 128, 5).transpose(1, 0, 2))
    Wall = np.concatenate([P["ada_w"][0], P["ada_w"][1]], 1)
    Ball = np.concatenate([P["ada_b"][0], P["ada_b"][1]], 0)
    ims = [{"cv": cv, "w": np.ascontiguousarray(Wall[:, c * 3072:(c + 1) * 3072]),
            "bias": np.ascontiguousarray(np.broadcast_to(Ball[c * 3072:(c + 1) * 3072], (5, 3072)))} for c in range(8)]
    res = run(get_prog("mods", build_mods), ims)
    return np.concatenate([r["out"] for r in res], 1).reshape(5, 2, 6, D)


def kernel(**inputs):
    P = {k: np.asarray(v, dtype=np.float32) for k, v in inputs.items()}
    mods = run_mods(P)
    tabs = rope_tables()
    consts = rwkv_consts()
    xT = [np.ascontiguousarray(np.concatenate([P["ctx"][b], P["x"][b]], 0).T) for b in range(NB)]
    for l in range(2):
        res = run(get_prog("la", build_la), [la_inputs(P, l, b, h, xT[b], mods) for b in range(NB) for h in range(2)])
        zT = [np.concatenate([res[2 * b]["zT"], res[2 * b + 1]["zT"]], 1) for b in range(NB)]
        del res
        res = run(get_prog("mla", build_mla), [mla_inputs(zT[b], P, l, hg, tabs) for b in range(NB) for hg in range(2)])
        oa = [np.concatenate([res[2 * b]["oaT"], res[2 * b + 1]["oaT"]], 0) for b in range(NB)]
        del res
        res = run(get_prog("rwkv", build_rwkv), [rwkv_inputs(zT[b], P, l, d, consts) for b in range(NB) for d in range(2)])
        rw = []
        for b in range(NB):
            f, r = res[2 * b], res[2 * b + 1]
            rw.append(np.stack([ytok_to_fm(f["ytok"]), seg_rev(ytok_to_fm(r["ytok"])), f["convT"][0], f["convT"][1], f["convT"][2],
                                f["scanT"][4], seg_rev(r["scanT"][4])]))
        del res
        res = run(get_prog("s5", build_s5), [s5_inputs(zT[b], P, l, d) for b in range(NB) for d in range(2)])
        ys = [(res[2 * b]["ysT"], seg_rev(res[2 * b + 1]["ysT"])) for b in range(NB)]
        del res
        res = run(get_prog("lc", build_lc), [lc_inputs(P, l, b, h, xT[b], zT[b], oa[b], rw[b], ys[b], mods, consts) for b in range(NB) for h in range(2)])
        xT = [np.concatenate([res[2 * b]["outT"], res[2 * b + 1]["outT"]], 1) for b in range(NB)]
        del res, zT, oa, rw, ys
    out = np.stack([np.ascontiguousarray(xT[b][:, CTX:].T) for b in range(NB)])
    return out.astype(np.float32)
```
